# Optimizing a Trainium2 kernel written in Bass

```python
import math
import jax, jax.numpy as jnp
from jax import lax
import numpy as np

D_MODEL = 1024
BATCH = 2
SEQ = 8192
DEPTH = 1
DEC_BATCH = 128
DEC_SEQ = 1
PAST_LEN = 16384
PAGE_SIZE = 128

HEAD_DIM = 64
N_HEADS = D_MODEL // HEAD_DIM
N_KV_HEADS = N_HEADS // 4
GROUP = N_HEADS // N_KV_HEADS
ROT_DIMS = HEAD_DIM // 4
ROPE_THETA = 500000.0
WINDOW = 128
BLOCK = 128
POOL_WIDTH = D_MODEL // 2
POOL_WINDOWS = (2, 4, 8, 16)
POOL_GROUPS = len(POOL_WINDOWS)
POOL_GC = POOL_WIDTH // POOL_GROUPS
POOL_STATE = max(POOL_WINDOWS) - 1
FFN_HIDDEN = -(-8 * D_MODEL // (3 * 256)) * 256
PLE_DIM = 256
EPS = 1e-6
NEG_INF = -1e30

Q_W = N_HEADS * HEAD_DIM
KV_W = N_KV_HEADS * HEAD_DIM
IN_COLS = POOL_WIDTH + Q_W + 2 * KV_W + 2 * D_MODEL
SPLITS = (POOL_WIDTH, POOL_WIDTH + Q_W, POOL_WIDTH + Q_W + KV_W, POOL_WIDTH + Q_W + 2 * KV_W, POOL_WIDTH + Q_W + 2 * KV_W + D_MODEL)

kernel_name = "hybrid_pool_swa_sink_decoder_step"


def _rms_norm(x, g):
    xf = x.astype(jnp.float32)
    y = xf * lax.rsqrt(jnp.mean(xf * xf, axis=-1, keepdims=True) + EPS)
    return (y * g.astype(jnp.float32)).astype(x.dtype)


def _partial_rope(x, pos):
    half = ROT_DIMS // 2
    inv = ROPE_THETA ** (-(jnp.arange(0, ROT_DIMS, 2, dtype=jnp.float32) / ROT_DIMS))
    ang = pos.astype(jnp.float32)[:, None] * inv[None, :]
    cos = jnp.cos(ang)[None, :, None, :]
    sin = jnp.sin(ang)[None, :, None, :]
    xr = x[..., :ROT_DIMS].astype(jnp.float32)
    x1, x2 = xr[..., :half], xr[..., half:]
    rot = jnp.concatenate([x1 * cos - x2 * sin, x2 * cos + x1 * sin], axis=-1)
    return jnp.concatenate([rot.astype(x.dtype), x[..., ROT_DIMS:]], axis=-1)


def _layer_inputs(h, ln1, w_in, pos):
    B, T, _ = h.shape
    z = _rms_norm(h, ln1) @ w_in
    u, q, k, v, gp, ga = jnp.split(z, SPLITS, axis=-1)
    q = _partial_rope(q.reshape(B, T, N_HEADS, HEAD_DIM), pos)
    k = _partial_rope(k.reshape(B, T, N_KV_HEADS, HEAD_DIM), pos)
    v = v.reshape(B, T, N_KV_HEADS, HEAD_DIM)
    return u, q, k, v, gp, ga


def _pool_branch(u_prev, u, pos, group_w, scale):
    T = u.shape[1]
    P = POOL_STATE
    ext = jnp.concatenate([u_prev, u], axis=1)
    extf = ext.astype(jnp.float32)
    cs = jnp.concatenate([jnp.zeros_like(extf[:, :1]), jnp.cumsum(extf, axis=1)], axis=1)
    means = []
    for g, w in enumerate(POOL_WINDOWS):
        c0, c1 = g * POOL_GC, (g + 1) * POOL_GC
        s = cs[:, P + 1:P + 1 + T, c0:c1] - cs[:, P + 1 - w:P + 1 - w + T, c0:c1]
        cnt = jnp.minimum(w, pos + 1).astype(jnp.float32)[None, :, None]
        means.append(s / cnt)
    m = (jnp.concatenate(means, axis=-1) - u.astype(jnp.float32)).astype(u.dtype)
    B = u.shape[0]
    mixed = jnp.einsum('btgc,gcd->btgd', m.reshape(B, T, POOL_GROUPS, POOL_GC), group_w)
    mixed = mixed.reshape(B, T, POOL_WIDTH) * scale
    return mixed, ext[:, -P:]


def _sink_attn(q, k, v, mask, sinks):
    s = jnp.einsum('...qkgd,...skd->...kgqs', q.astype(jnp.float32), k.astype(jnp.float32)) * (HEAD_DIM ** -0.5)
    s = jnp.where(mask, s, jnp.float32(NEG_INF))
    sink = jnp.broadcast_to(sinks.astype(jnp.float32)[:, :, None, None], s.shape[:-1] + (1,))
    pr = jax.nn.softmax(jnp.concatenate([s, sink], axis=-1), axis=-1)[..., :-1]
    o = jnp.einsum('...kgqs,...skd->...qkgd', pr, v.astype(jnp.float32))
    return o.astype(q.dtype)


def _attn_prompt(q, k, v, sinks):
    B, S = q.shape[0], q.shape[1]
    nb = S // BLOCK
    qb = q.reshape(B, nb, BLOCK, N_KV_HEADS, GROUP, HEAD_DIM)
    kb = k.reshape(B, nb, BLOCK, N_KV_HEADS, HEAD_DIM)
    vb = v.reshape(B, nb, BLOCK, N_KV_HEADS, HEAD_DIM)
    kk = jnp.concatenate([jnp.concatenate([jnp.zeros_like(kb[:, :1]), kb[:, :-1]], axis=1), kb], axis=2)
    vv = jnp.concatenate([jnp.concatenate([jnp.zeros_like(vb[:, :1]), vb[:, :-1]], axis=1), vb], axis=2)
    qi = jnp.arange(BLOCK)[:, None]
    si = jnp.arange(2 * BLOCK)[None, :]
    rel = qi + BLOCK - si
    kpos = (jnp.arange(nb)[:, None, None] - 1) * BLOCK + si[None]
    mask = (rel >= 0)[None] & (rel < WINDOW)[None] & (kpos >= 0)
    o = _sink_attn(qb, kk, vv, mask[:, None, None], sinks.reshape(N_KV_HEADS, GROUP))
    return o.reshape(B, S, Q_W)


def _attn_sample(q, k, v, cache_k, cache_v, sinks):
    Bd, T = q.shape[0], q.shape[1]
    w_cache = cache_k.shape[1]
    kk = jnp.concatenate([cache_k, k], axis=1)
    vv = jnp.concatenate([cache_v, v], axis=1)
    qpos = PAST_LEN + jnp.arange(T)
    kpos = jnp.concatenate([PAST_LEN - w_cache + jnp.arange(w_cache), qpos])
    rel = qpos[:, None] - kpos[None, :]
    mask = (rel >= 0) & (rel < WINDOW)
    o = _sink_attn(q.reshape(Bd, T, N_KV_HEADS, GROUP, HEAD_DIM), kk, vv, mask, sinks.reshape(N_KV_HEADS, GROUP))
    return o.reshape(Bd, T, Q_W), kk[:, -w_cache:], vv[:, -w_cache:]


def _layer_outputs(h, p, pooled, attn, gp, ga, w_pool_branch, w_attn_branch, w_out, ln2, w_ffn_in, w_ffn_out, w_ple_proj, ple_norm, w_ple_gate):
    merged = jax.nn.sigmoid(gp) * (pooled @ w_pool_branch) + jax.nn.sigmoid(ga) * (attn @ w_attn_branch)
    h = h + merged @ w_out
    gate, up = jnp.split(_rms_norm(h, ln2) @ w_ffn_in, 2, axis=-1)
    h = h + (jax.nn.silu(gate) * up) @ w_ffn_out
    e = _rms_norm(p @ w_ple_proj, ple_norm)
    return h + jax.nn.sigmoid(h @ w_ple_gate) * e


def setup_inputs(seed: int = 0) -> dict:
    key = jax.random.key(seed)
    ks = jax.random.split(key, 32)
    f32 = jnp.float32
    w_cache = min(WINDOW, PAST_LEN)
    nrm = lambda k, shape, s=1.0: jax.random.normal(k, shape, f32) * s
    return {
        "x_prompt": nrm(ks[0], (BATCH, SEQ, D_MODEL)),
        "x_sample": nrm(ks[1], (DEC_BATCH, DEC_SEQ, D_MODEL)),
        "p_prompt": nrm(ks[2], (DEPTH, BATCH, SEQ, PLE_DIM)),
        "p_sample": nrm(ks[3], (DEPTH, DEC_BATCH, DEC_SEQ, PLE_DIM)),
        "cache_k": nrm(ks[4], (DEPTH, DEC_BATCH, w_cache, N_KV_HEADS, HEAD_DIM)),
        "cache_v": nrm(ks[5], (DEPTH, DEC_BATCH, w_cache, N_KV_HEADS, HEAD_DIM)),
        "state_pool": nrm(ks[6], (DEPTH, DEC_BATCH, POOL_STATE, POOL_WIDTH)),
        "ln1": 1.0 + nrm(ks[7], (DEPTH, D_MODEL), 0.02),
        "w_in": nrm(ks[8], (DEPTH, D_MODEL, IN_COLS), D_MODEL ** -0.5),
        "pool_group_w": nrm(ks[9], (DEPTH, POOL_GROUPS, POOL_GC, POOL_GC), POOL_GC ** -0.5),
        "pool_scale": 1.0 + nrm(ks[10], (DEPTH, POOL_WIDTH), 0.02),
        "attn_sinks": nrm(ks[11], (DEPTH, N_HEADS), 0.5),
        "w_pool_branch": nrm(ks[12], (DEPTH, POOL_WIDTH, D_MODEL), POOL_WIDTH ** -0.5),
        "w_attn_branch": nrm(ks[13], (DEPTH, Q_W, D_MODEL), Q_W ** -0.5),
        "w_out": nrm(ks[14], (DEPTH, D_MODEL, D_MODEL), D_MODEL ** -0.5),
        "ln2": 1.0 + nrm(ks[15], (DEPTH, D_MODEL), 0.02),
        "w_ffn_in": nrm(ks[16], (DEPTH, D_MODEL, 2 * FFN_HIDDEN), D_MODEL ** -0.5),
        "w_ffn_out": nrm(ks[17], (DEPTH, FFN_HIDDEN, D_MODEL), FFN_HIDDEN ** -0.5),
        "w_ple_proj": nrm(ks[18], (DEPTH, PLE_DIM, D_MODEL), PLE_DIM ** -0.5),
        "ple_norm": 1.0 + nrm(ks[19], (DEPTH, D_MODEL), 0.02),
        "w_ple_gate": nrm(ks[20], (DEPTH, D_MODEL, D_MODEL), D_MODEL ** -0.5),
        "final_norm": 1.0 + nrm(ks[21], (D_MODEL,), 0.02),
    }


def reference(x_prompt, x_sample, p_prompt, p_sample, cache_k, cache_v, state_pool, ln1, w_in, pool_group_w, pool_scale, attn_sinks, w_pool_branch, w_attn_branch, w_out, ln2, w_ffn_in, w_ffn_out, w_ple_proj, ple_norm, w_ple_gate, final_norm):
    B, S, _ = x_prompt.shape
    T = x_sample.shape[1]
    w_cache = cache_k.shape[2]
    pos_p = jnp.arange(S, dtype=jnp.int32)
    pos_s = PAST_LEN + jnp.arange(T, dtype=jnp.int32)
    hp, hs = x_prompt, x_sample
    nkp, nvp, npp, nks, nvs, nps = [], [], [], [], [], []
    for i in range(DEPTH):
        u, q, k, v, gp, ga = _layer_inputs(hp, ln1[i], w_in[i], pos_p)
        pooled, st = _pool_branch(jnp.zeros((B, POOL_STATE, POOL_WIDTH), u.dtype), u, pos_p, pool_group_w[i], pool_scale[i])
        att = _attn_prompt(q, k, v, attn_sinks[i])
        hp = _layer_outputs(hp, p_prompt[i], pooled, att, gp, ga, w_pool_branch[i], w_attn_branch[i], w_out[i], ln2[i], w_ffn_in[i], w_ffn_out[i], w_ple_proj[i], ple_norm[i], w_ple_gate[i])
        nkp.append(k[:, -w_cache:])
        nvp.append(v[:, -w_cache:])
        npp.append(st)
        u, q, k, v, gp, ga = _layer_inputs(hs, ln1[i], w_in[i], pos_s)
        pooled, st = _pool_branch(state_pool[i], u, pos_s, pool_group_w[i], pool_scale[i])
        att, kn, vn = _attn_sample(q, k, v, cache_k[i], cache_v[i], attn_sinks[i])
        hs = _layer_outputs(hs, p_sample[i], pooled, att, gp, ga, w_pool_branch[i], w_attn_branch[i], w_out[i], ln2[i], w_ffn_in[i], w_ffn_out[i], w_ple_proj[i], ple_norm[i], w_ple_gate[i])
        nks.append(kn)
        nvs.append(vn)
        nps.append(st)
    y_prompt = _rms_norm(hp, final_norm)
    y_sample = _rms_norm(hs, final_norm)
    return (y_prompt, y_sample, jnp.stack(nkp), jnp.stack(nvp), jnp.stack(npp), jnp.stack(nks), jnp.stack(nvs), jnp.stack(nps))
```

```python
import numpy as np
import ml_dtypes
from contextlib import ExitStack
import concourse.bass as bass
import concourse.mybir as mybir
from concourse.bass_utils import run_bass_kernel_spmd

F32 = mybir.dt.float32
BF16 = mybir.dt.bfloat16
AF = mybir.ActivationFunctionType
ALU = mybir.AluOpType

NCORE = 8
D = 1024
TOKC = 2048
NT = 4
NSMP = 16
FF = 2816
NHC = 22
EPS = 1e-6
RING = 4
ENGS = ("pe", "act", "dve", "pool", "sp")
DO_SAMPLE = True
STOP = None
EVAC_DVE = False


class Buf:
    __slots__ = ("name", "w", "rs")

    def __init__(self, name):
        self.name = name
        self.w = None
        self.rs = []


class DSem:
    def __init__(self, h):
        self.h = h
        self.count = 0


class Op:
    __slots__ = ("eng", "fn", "deps", "sig", "sem", "val", "dma", "epoch", "ndma")


class Sched:
    def __init__(self):
        self.ops = {e: [] for e in ENGS}
        self.epoch = 0
        self.all = []

    def add(self, eng, fn, r=(), w=(), dsem=None, ndma=1):
        op = Op()
        op.eng, op.fn, op.epoch = eng, fn, self.epoch
        op.dma = dsem is not None
        op.ndma = ndma
        op.sig = op.dma
        op.sem = dsem
        op.val = None
        if op.dma:
            dsem.count += 16 * ndma
            op.val = dsem.count
        deps = []
        for b in r:
            if b.w is not None:
                deps.append(b.w)
        for b in w:
            if b.w is not None:
                deps.append(b.w)
            deps.extend(b.rs)
        for b in r:
            b.rs.append(op)
        for b in w:
            b.w = op
            b.rs = []
        seen = set()
        op.deps = []
        for d in deps:
            if d is op or id(d) in seen:
                continue
            seen.add(id(d))
            if d.eng == "pe" and eng == "pe" and not d.dma and not op.dma:
                continue
            op.deps.append(d)
            d.sig = True
        self.ops[eng].append(op)
        return op

    def finalize(self, sems):
        for eng in ENGS:
            cnt = {}
            for op in self.ops[eng]:
                if op.dma or not op.sig:
                    continue
                c = cnt.get(op.epoch, 0) + 1
                cnt[op.epoch] = c
                op.sem = sems[eng][op.epoch]
                op.val = c

    def emit(self, eng, e):
        known = {}
        for op in self.ops[eng]:
            need = {}
            for d in op.deps:
                h = d.sem.h if d.dma else d.sem
                key = id(h)
                if key not in need or need[key][1] < d.val:
                    need[key] = (h, d.val)
            for key, (h, val) in need.items():
                if known.get(key, 0) >= val:
                    continue
                e.wait_ge(h, val)
                known[key] = val
            ins = op.fn(e)
            if op.sig:
                if op.dma:
                    lst = ins if isinstance(ins, (list, tuple)) else [ins]
                    assert len(lst) == op.ndma
                    for i_ in lst:
                        i_.then_inc(op.sem.h, 16)
                else:
                    ins.then_inc(op.sem, 1)


def build_program():
    nc = bass.Bass("TRN2", target_bir_lowering=False)

    def din(name, shape, dt=F32):
        return nc.dram_tensor(name, list(shape), dt, kind="ExternalInput").ap()

    def dout(name, shape, dt=F32):
        return nc.dram_tensor(name, list(shape), dt, kind="ExternalOutput").ap()

    xh = din("xh", [TOKC + 128, D])
    ph = din("ph", [TOKC, 256])
    xs = din("xs", [NSMP, D])
    ps = din("ps", [NSMP, 256])
    ck = din("ck", [NSMP, 128, 256])
    cv = din("cv", [NSMP, 128, 256])
    spool = din("spool", [NSMP, 15, 512])
    ln1 = din("ln1", [D])
    w_in = din("w_in", [D, 4096])
    pgw = din("pool_group_w", [4, 128, 128])
    pscale_d = din("pool_scale", [512])
    sinks_d = din("attn_sinks", [16])
    w_pb = din("w_pool_branch", [512, D])
    w_ab = din("w_attn_branch", [D, D])
    w_out = din("w_out", [D, D])
    ln2 = din("ln2", [D])
    w_fi = din("w_ffn_in", [D, 2 * FF])
    w_fo = din("w_ffn_out", [FF, D])
    w_pp = din("w_ple_proj", [256, D])
    plen = din("ple_norm", [D])
    w_pg = din("w_ple_gate", [D, D])
    fnorm = din("final_norm", [D])
    cs_d = din("cs", [128, 2 * 18 * 8])
    mask_d = din("maskd", [128, 2, 1024], BF16)
    rc16_d = din("rc16", [128, 64])
    ident_d = din("ident", [128, 128], BF16)
    identf_d = din("identf", [128, 128], F32)
    sel_d = din("sel", [128, 128], F32)

    y = dout("y", [TOKC, D])
    ys = dout("ys", [NSMP, D])
    nk = dout("nk", [128, 256])
    nv = dout("nv", [128, 256])
    npool = dout("npool", [16, 512])
    nks = dout("nks", [NSMP, 128, 256])
    nvs = dout("nvs", [NSMP, 128, 256])
    nps = dout("nps", [NSMP, 15, 512])

    NU_ = 33
    wscr = nc.dram_tensor("wscr", [NU_, 128, 4096], BF16, kind="Internal").ap()
    S = Sched()
    es = ExitStack()
    with es:
        def sb(name, shape, dt):
            return es.enter_context(nc.sbuf_tensor(name, list(shape), dt))

        x_tm = sb("x_tm", [128, 8, D], F32)
        x_aux = sb("x_aux", [128, D], F32)
        tmst = sb("tmst", [128, 4, D], BF16)
        xnT = sb("xnT", [128, 8, 640], BF16)
        u_ext = sb("u_ext", [128, 4, 528], F32)
        ptmp = sb("ptmp", [128, 2, 528], F32)
        m_sb = sb("m_sb", [128, 4, 528], BF16)
        mixT = sb("mixT", [128, 4, 528], BF16)
        qT = sb("qT", [128, 8, 528], BF16)
        kT = sb("kT", [128, 4, 640], BF16)
        vaug = sb("vaug", [128, 5, 4, 65], BF16)
        sg = sb("sg", [128, 16, 528], BF16)
        PT = sb("PT", [128, 2, 1024], BF16)
        masks = sb("masks", [128, 2, 1024], BF16)
        mrgT = sb("mrgT", [128, 8, 528], BF16)
        tmpf = sb("tmpf", [128, 2, 528], F32)
        p_tm = sb("p_tm", [128, 5, 256], F32)
        pT = sb("pT", [128, 2, 640], BF16)
        ework = sb("ework", [128, 2, D], F32)
        gate_tm = sb("gate_tm", [128, D], BF16)
        gains = sb("gains", [128, 4, D], F32)
        cs_sb = sb("cs_sb", [128, 2, 18, 8], F32)
        ring = sb("ring", [128, RING, 4096], BF16)
        scal = sb("scal", [128, 96], F32)
        ident = sb("ident_sb", [128, 128], BF16)
        identf = sb("identf_sb", [128, 128], F32)
        rc16 = sb("rc16_sb", [128, 4, 16], F32)
        pscale = sb("pscale", [128, 4], F32)
        es_sb = sb("es_sb", [128, 16], F32)
        Gw = sb("Gw", [128, 4, 128], BF16)
        kf = sb("kf", [128, 2, 512], F32)
        k_tm = sb("k_tm", [128, 2, 512], BF16)
        ropet = sb("ropet", [128, 4, 128], F32)
        ropeq = sb("ropeq", [128, 16, 16], F32)
        dn = sb("dn", [128, 2, 8], F32)
        Osb = sb("Osb", [128, 2, 256], F32)
        dummy = sb("dummy_sb", [128, 2], F32)
        nhalf = sb("nhalf", [128, 1], F32)
        ones_bf = sb("ones_bf", [128, 128], BF16)
        pp = [es.enter_context(nc.psum_tensor(f"pp{i}", [128, 1024], F32)) for i in range(4)]

        attn_tm = ework[:].rearrange("p a d -> p (a d)").bitcast(BF16).rearrange("p (a d) -> p a d", d=D)

        xflat = x_tm[:, 0:4, :].rearrange("p a d -> p (a d)")
        def bfv(ap_):
            return ap_.bitcast(BF16)
        cks_g = bfv(xflat[:, 0:512]).rearrange("p (s c) -> p s c", c=256)
        cvs_g = bfv(xflat[:, 512:1024]).rearrange("p (s c) -> p s c", c=256)
        ksT_g = bfv(xflat[:, 1024:1536]).rearrange("p (c k) -> p c k", k=128)
        st2 = xflat[:, 1536:2560].rearrange("p (h c) -> p h c", c=512)
        cks_B = bfv(xflat[:, 1536:2048]).rearrange("p (s c) -> p s c", c=256)
        cvs_B = bfv(xflat[:, 2048:2560]).rearrange("p (s c) -> p s c", c=256)
        us_tm = xflat[:16, 2560:3072]
        ssum = xflat[:16, 3072:3584]
        ms_bf = bfv(xflat[:16, 3584:3840])
        knew = bfv(xflat[:16, 3840:4096])
        gflat = gate_tm[:, :].bitcast(F32)
        PTs = bfv(gflat[:, 0:128])
        rds = gflat[:, 128:384]
        OTsb = gflat[:, 384:512]
        pflat = pT[:].rearrange("p a d -> p (a d)").bitcast(F32)
        qsh = bfv(pflat[:, 0:56]).rearrange("p (c t) -> p c t", t=16)
        qsel = bfv(pflat[:, 64:192]).rearrange("p (s h) -> p s h", h=16)
        sel = pflat[:, 192:320]

        def bank(b):
            return pp[b // 2][:, (b % 2) * 512:(b % 2) * 512 + 512]

        def bank_bf(b):
            return bank(b).bitcast(BF16)

        def pair(b):
            return pp[b // 2][:, :]

        BK = [Buf(f"bank{i}") for i in range(8)]

        class Banks:
            i = 0
            reserved = ()

            def one(self):
                while True:
                    b = self.i % 8
                    self.i += 1
                    if b not in self.reserved:
                        return b

            def pair(self):
                while True:
                    if self.i % 2:
                        self.i += 1
                    b = self.i % 8
                    self.i += 2
                    if b not in self.reserved and (b + 1) not in self.reserved:
                        return b
        banks = Banks()

        B_x = [Buf(f"x{i}") for i in range(8)]
        B_xaux = Buf("xaux")
        B_tm = [Buf(f"tm{i}") for i in range(4)]
        B_xnT = [Buf(f"xnT{i}") for i in range(5)]
        B_u = Buf("u")
        B_pt = [Buf("ptA"), Buf("ptB")]
        B_m = [Buf(f"m{g}") for g in range(4)]
        B_mix = [Buf(f"mix{g}") for g in range(4)]
        B_qT = [Buf(f"qT{i}") for i in range(5)]
        B_kT = [Buf(f"kT{i}") for i in range(5)]
        B_v = [Buf(f"v{i}") for i in range(5)]
        B_sg = [Buf(f"sg{i}") for i in range(16)]
        B_PT = [Buf("PT0"), Buf("PT1")]
        B_mrg = [Buf(f"mrg{i}") for i in range(8)]
        B_tmpf = [Buf("tmpf0"), Buf("tmpf1")]
        B_p = [Buf(f"p{i}") for i in range(5)]
        B_pT = [Buf(f"pT{i}") for i in range(5)]
        B_ew = [Buf("ew0"), Buf("ew1")]
        B_gate = Buf("gate")
        B_const = Buf("const")
        B_ring = [Buf(f"ring{i}") for i in range(RING)]
        B_kf = [Buf("kf0"), Buf("kf1")]
        B_ktm = [Buf("ktm0"), Buf("ktm1")]
        B_rope = Buf("rope")
        B_ropeq = Buf("ropeq")
        B_dn = [Buf("dn0"), Buf("dn1")]
        B_Osb = [Buf("Osb0"), Buf("Osb1")]
        B_out = Buf("out")
        B_scal = [Buf(f"scal{i}") for i in range(32)]
        B_smp = Buf("smp")
        SB = {k: Buf("s_" + k) for k in ("cks", "cvs", "cksB", "cvsB", "ksT", "st2", "us", "ssum", "ms", "knew", "PTs", "rds", "OTs", "qsh", "qsel", "sel")}

        nsem_epochs = NT + 1
        sems = {e: [es.enter_context(nc.semaphore(f"s_{e}{i}")) for i in range(nsem_epochs)] for e in ENGS}

        def dsem(name):
            return DSem(es.enter_context(nc.semaphore(name)))
        ds_ring = [dsem(f"d_ring{i}") for i in range(RING)]
        ds_ringh = [dsem(f"d_ringh{i}") for i in range(RING)]
        ds_wb = [dsem(f"d_wb{i}") for i in range(RING)]
        B_scr = [Buf(f"scr{i}") for i in range(NU_)]
        ds_x = [dsem(f"d_x{i}") for i in range(8)]
        ds_xaux = dsem("d_xaux")
        ds_p = [dsem(f"d_p{i}") for i in range(5)]
        ds_const = dsem("d_const")
        ds_yo = [dsem(f"d_yo{i}") for i in range(9)]
        ds_misc = dsem("d_misc")
        ds_gw = dsem("d_gw")
        ds_npool = dsem("d_npool")
        ds_smp = dsem("d_smp")
        ds_ck = [dsem("d_ck"), dsem("d_ckB")]
        ds_row = [dsem("d_row"), dsem("d_rowB")]
        ds_st = dsem("d_st")
        ds_nps = dsem("d_nps")
        ds_nkvs = dsem("d_nkvs")

        A = S.add
        scal_i = [0]

        def nscal():
            i = scal_i[0] % 32
            scal_i[0] += 1
            return i

        B_ca = Buf("constA")
        ds_ca = dsem("d_ca")

        def ld_const_a(e):
            L = []
            L.append(e.dma_start(out=ident[:], in_=ident_d))
            L.append(e.dma_start(out=gains[:, 0, :], in_=ln1.partition_broadcast(128)))
            return L

        def ld_const(e):
            L = []
            L.append(e.dma_start(out=identf[:], in_=identf_d))
            L.append(e.dma_start(out=masks[:], in_=mask_d))
            L.append(e.dma_start(out=rc16[:].rearrange("p g c -> p (g c)"), in_=rc16_d))
            L.append(e.dma_start(out=gains[:, 1, :], in_=ln2.partition_broadcast(128)))
            L.append(e.dma_start(out=gains[:, 2, :], in_=plen.partition_broadcast(128)))
            L.append(e.dma_start(out=gains[:, 3, :], in_=fnorm.partition_broadcast(128)))
            L.append(e.dma_start(out=cs_sb[:].rearrange("p a b c -> p (a b c)"), in_=cs_d))
            for g in range(4):
                L.append(e.dma_start(out=pscale[:, g:g + 1], in_=pscale_d[g * 128:(g + 1) * 128].rearrange("(p o) -> p o", o=1)))
            L.append(e.dma_start(out=es_sb[:], in_=sinks_d.partition_broadcast(128)))
            return L
        A("sp", ld_const_a, w=[B_ca], dsem=ds_ca, ndma=2)
        A("pool", lambda e: [e.dma_start(out=Gw[:], in_=pgw.rearrange("g c d -> c g d"))], w=[B_const], dsem=ds_gw, ndma=1)
        A("pool", lambda e: e.memset(vaug[:, :, :, 64:65], 1.0), w=B_v)
        A("pool", lambda e: e.memset(ones_bf[:], 1.0), w=[B_const])
        A("pool", lambda e: e.memset(nhalf[:], -0.5), w=[B_ca])

        unit_ctr = [0]

        def wload(srcs):
            slot = unit_ctr[0] % RING
            unit_ctr[0] += 1

            def fn(e, srcs=srcs, slot=slot):
                L = []
                for (src, coff, ncols, nkc, width) in srcs:
                    dst = ring[:, slot, 0:nkc * width].rearrange("p (k c) -> p k c", c=width)[:, :, coff:coff + ncols]
                    L.append(e.dma_start(out=dst, in_=src.rearrange("(k p) c -> p k c", p=128)))
                return L
            A("pool", fn, r=(B_x[0:4] if 0 < unit_ctr[0] - 1 < RING else []), w=[B_ring[slot]], dsem=ds_ring[slot], ndma=len(srcs))
            return slot

        def rview(slot, nkc, width):
            return ring[:, slot, 0:nkc * width].rearrange("p (k c) -> p k c", c=width)

        def unit_list():
            U = []
            for c0 in (0, 2048, 2560, 3072, 3584, 512, 1024, 1536):
                U.append(("win", c0))
            U.append(("wpb",))
            U.append(("wab", 0)); U.append(("wab", 512))
            U.append(("wout", 0)); U.append(("wout", 512))
            for n in range(11):
                U.append(("ffi", n))
            for half in range(2):
                for k0 in (0, 8, 16):
                    U.append(("ffo", half, k0))
            U.append(("wpg", 0)); U.append(("wpg", 512))
            U.append(("wpp",))
            return U
        UL = unit_list()
        NU = len(UL)
        issued = [0]
        slot_of = {}

        def issue_unit(gidx):
            un = gidx % NU
            if gidx >= NU:
                slot = unit_ctr[0] % RING
                unit_ctr[0] += 1
                A("sp", lambda e, slot=slot, un=un: [e.dma_start(out=ring[:, slot, :], in_=wscr[un])],
                  r=[B_scr[un]], w=[B_ring[slot]], dsem=ds_ringh[slot])
                slot_of[gidx] = slot
                return
            u = UL[gidx % NU]
            if u[0] == "win":
                s = wload([(w_in[:, u[1]:u[1] + 512], 0, 512, 8, 512)])
            elif u[0] == "wpb":
                s = wload([(w_pb[:, :], 0, 1024, 4, 1024)])
            elif u[0] == "wab":
                s = wload([(w_ab[:, u[1]:u[1] + 512], 0, 512, 8, 512)])
            elif u[0] == "wout":
                s = wload([(w_out[:, u[1]:u[1] + 512], 0, 512, 8, 512)])
            elif u[0] == "ffi":
                n = u[1]
                s = wload([(w_fi[:, n * 256:n * 256 + 256], 0, 256, 8, 512),
                           (w_fi[:, FF + n * 256:FF + n * 256 + 256], 256, 256, 8, 512)])
            elif u[0] == "ffo":
                half, k0 = u[1], u[2]
                nkc = min(8, NHC - k0)
                s = wload([(w_fo[k0 * 128:(k0 + nkc) * 128, half * 512:half * 512 + 512], 0, 512, nkc, 512)])
            elif u[0] == "wpg":
                s = wload([(w_pg[:, u[1]:u[1] + 512], 0, 512, 8, 512)])
            elif u[0] == "wpp":
                s = wload([(w_pp[:, :], 0, 1024, 2, 1024)])
            slot_of[gidx] = s
            if NT > 1:
                A("sp", lambda e, s=s, un=un: [e.dma_start(out=wscr[un], in_=ring[:, s, :])], r=[B_ring[s]], w=[B_scr[un]], dsem=ds_wb[s])

        assert NU == NU_
        total_units = NU * NT

        released = [0]

        def fill():
            while issued[0] < min(total_units, released[0] + RING):
                issue_unit(issued[0])
                issued[0] += 1

        def release(k):
            released[0] += k
            fill()

        def use_unit(gidx):
            fill()
            assert gidx < issued[0], (gidx, issued[0], released[0])
            return slot_of[gidx]

        def rmsnorm_to_bf16(src_ap, srcbuf, nt, gidx, dst_ap, dstbuf):
            si = nscal()
            c = si * 3
            A("act", lambda e: e.activation(out=dst_ap, in_=src_ap, func=AF.Square, accum_out=scal[:nt, c:c + 1]),
              r=[srcbuf], w=[dstbuf, B_scal[si]])
            A("dve", lambda e: e.tensor_scalar(out=scal[:nt, c + 1:c + 2], in0=scal[:nt, c:c + 1], scalar1=1.0 / D, scalar2=EPS, op0=ALU.mult, op1=ALU.add),
              r=[], w=[B_scal[si]])
            A("pool", lambda e: e.tensor_tensor(out=scal[:nt, c + 2:c + 3], in0=scal[:nt, c + 1:c + 2], in1=nhalf[:nt, :], op=ALU.pow), r=[B_ca], w=[B_scal[si]])
            A("dve", lambda e: e.scalar_tensor_tensor(out=dst_ap, in0=src_ap, scalar=scal[:nt, c + 2:c + 3], in1=gains[:nt, gidx, :],
                                                       op0=ALU.mult, op1=ALU.mult),
              r=[srcbuf, B_scal[si], (B_ca if gidx == 0 else B_const)], w=[dstbuf])

        def transpose_to(src_fn, srcbuf, nt, nch, dst_ap, dstbufs, evac="act"):
            if STOP == 'x1':
                return
            b = banks.one()
            pv = bank_bf(b)

            def fn(e):
                ins = None
                for c in range(nch):
                    ins = e.transpose(out=pv[:, c * 128:c * 128 + nt], in_=src_fn(c), identity=ident[:nt, :nt])
                return ins
            A("pe", fn, r=(list(srcbuf) if isinstance(srcbuf, (list, tuple)) else [srcbuf]) + [B_ca], w=[BK[b]])
            src = pv[:, 0:nch * 128].rearrange("p (c t) -> p c t", t=128)[:, :, 0:nt]
            if STOP == 'x2':
                return
            ev_i[0] += 1
            if EVAC_DVE and nt == 128 and ev_i[0] % 2 == 0:
                evac = "dve"
            if evac == "dve":
                A("dve", lambda e: e.tensor_copy(out=dst_ap, in_=src), r=[BK[b]], w=dstbufs)
            else:
                A("act", lambda e: e.activation(out=dst_ap, in_=src, func=AF.Copy), r=[BK[b]], w=dstbufs)

        def transpose_f32_to(src_fn, srcbufs, nt, nch, dst_ap, dstbufs):
            if nch > 4:
                b = banks.pair()
                pv = pair(b)
                bks = [BK[b], BK[b + 1]]
            else:
                b = banks.one()
                pv = bank(b)
                bks = [BK[b]]

            def fn(e):
                ins = None
                for c in range(nch):
                    ins = e.transpose(out=pv[:, c * 128:c * 128 + nt], in_=src_fn(c), identity=identf[:nt, :nt])
                return ins
            A("pe", fn, r=list(srcbufs) + [B_const], w=bks)

            def ev(e):
                ins = None
                for c0 in range(0, nch, 4):
                    c1 = min(nch, c0 + 4)
                    ins = e.activation(out=dst_ap[:, c0:c1, :], in_=pv[:, c0 * 128:c1 * 128].rearrange("p (c t) -> p c t", t=128)[:, :, 0:nt], func=AF.Copy)
                return ins
            A("act", ev, r=bks, w=dstbufs)

        tm_i = [0]
        kf_i = [0]
        ev_i = [0]
        pre_tis = {}
        done_TX = {}
        done_aux = {}
        tail_ops = []
        ckb = []

        def ld_cast(g4):
            ckt, cvt, bk_, bv_ = ckb[g4 % 2]
            extra = [SB["st2"]] if g4 % 2 else []
            A("pool", lambda e, g4=g4, ckt=ckt, cvt=cvt: [e.dma_start(out=ckt, in_=ck[g4 * 4:(g4 + 1) * 4].rearrange("s k c -> k s c")),
                                                          e.dma_start(out=cvt, in_=cv[g4 * 4:(g4 + 1) * 4].rearrange("s k c -> k s c"))],
              r=[B_smp], w=[bk_, bv_] + extra, dsem=ds_ck[g4 % 2], ndma=2)

        def ld_row(g4):
            ckt, cvt, bk_, bv_ = ckb[g4 % 2]
            A("sp", lambda e, g4=g4, ckt=ckt, cvt=cvt: [e.dma_start(out=ckt[0:1, :, :], in_=knew[g4 * 4:(g4 + 1) * 4, 0:256]),
                                                        e.dma_start(out=cvt[0:1, :, :], in_=knew[g4 * 4:(g4 + 1) * 4, 256:512])],
              r=[SB["knew"]], w=[bk_, bv_], dsem=ds_row[g4 % 2], ndma=2)

        def ntm():
            i = tm_i[0] % 4
            tm_i[0] += 1
            return i

        gunit = [0]

        def next_unit():
            g = gunit[0]
            gunit[0] += 1
            return use_unit(g)

        def load_x_tile(t):
            slot = t % 2
            for b in range(4):
                r0 = 128 + t * 512 + b * 128
                A("sp", lambda e, r0=r0, i=slot * 4 + b: [e.dma_start(out=x_tm[:, i, :], in_=xh[r0:r0 + 128, :])],
                  w=[B_x[slot * 4 + b]], dsem=ds_x[slot * 4 + b])

        load_x_tile(0)
        A("sp", ld_const, w=[B_const], dsem=ds_const, ndma=12)
        A("sp", lambda e: [e.dma_start(out=x_aux[:, :], in_=xh[0:128, :])], w=[B_xaux], dsem=ds_xaux)

        def do_tile(t):
            S.epoch = t
            slot = t % 2
            last = (t == NT - 1)
            has_s = last and DO_SAMPLE
            blocks = []
            for b in range(4):
                blocks.append(("p", 128, x_tm[:, slot * 4 + b, :], B_x[slot * 4 + b], b * 128, b, 1 + t * 4 + b))
            if t == 0:
                blocks.append(("h", 128, x_aux[:, :], B_xaux, 512, 4, 0))
            if has_s:
                blocks.append(("s", NSMP, x_aux[:NSMP, :], B_xaux, 512, 4, 17))
            for b in range(4):
                r0 = t * 512 + b * 128
                A("sp", lambda e, r0=r0, b=b: [e.dma_start(out=p_tm[:, b, :], in_=ph[r0:r0 + 128, :])],
                  w=[B_p[b]], dsem=ds_p[b])
            if DO_SAMPLE and t == max(NT - 2, 0) and NT > 1:
                A("sp", lambda e: [e.dma_start(out=x_aux[:NSMP, :], in_=xs)], w=[B_xaux], dsem=ds_xaux)
                A("sp", lambda e: [e.dma_start(out=p_tm[:NSMP, 4, :], in_=ps)], w=[B_p[4]], dsem=ds_p[4])
            ncol = 528 if has_s else 512
            if STOP == 'c0':
                return

            for gi_, grp in enumerate((blocks[0:4], blocks[4:])):
                if gi_ == 1 and done_aux.get(t):
                    continue
                if gi_ == 0 and t in pre_tis:
                    tis = pre_tis[t]
                else:
                    tis = []
                    for (kind, nt, xap, xbuf, col0, cb, trow) in grp:
                        ti = ntm()
                        tis.append(ti)
                        rmsnorm_to_bf16(xap, xbuf, nt, 0, tmst[:nt, ti, :], B_tm[ti])
                if gi_ == 0 and done_TX.get(t):
                    continue
                for ti, (kind, nt, xap, xbuf, col0, cb, trow) in zip(tis, grp):
                    if kind == "h":
                        transpose_to(lambda c, ti=ti, nt=nt: tmst[:nt, ti, c * 128:(c + 1) * 128], B_tm[ti], nt, 8,
                                     xnT[:, :, col0:col0 + nt], [B_xnT[cb]])
                    else:
                        transpose_to(lambda c, ti=ti, nt=nt: tmst[:nt, ti, c * 128:(c + 1) * 128], B_tm[ti], nt, 8,
                                     qT[:, :, col0:col0 + nt], [B_qT[cb]])
            if STOP in ('x', 'x1', 'x2'):
                return
            if has_s or t == 0:
                while tail_ops:
                    tail_ops.pop(0)()
                if t + 1 < NT:
                    load_x_tile(t + 1)
            if has_s:
                while tail_ops:
                    tail_ops.pop(0)()
                A("pool", lambda e: e.memset(dummy[:, 0:1], 0.0), w=[B_smp] + list(SB.values()) + B_x[0:4] + [B_gate] + B_pT)
                A("sp", lambda e: [e.dma_start(out=sel, in_=sel_d),
                                   e.dma_start(out=st2[0:120, 0, :], in_=spool[0:8].rearrange("s r c -> (s r) c")),
                                   e.dma_start(out=st2[0:120, 1, :], in_=spool[8:16].rearrange("s r c -> (s r) c"))],
                  r=[B_smp], w=[SB["sel"], SB["st2"]], dsem=ds_st, ndma=3)
            xn_main = B_qT[0:4]
            xn_all = B_qT[0:4] + ([B_qT[4]] if has_s else [])
            hn_all = B_xnT[0:4] + ([B_xnT[4]] if has_s else [])

            s_u = next_unit()
            wv = rview(s_u, 8, 512)
            for g in range(4):
                b = banks.one()

                def fn(e, g=g, b=b, wv=wv):
                    ins = None
                    for kc in range(8):
                        ins = e.matmul(bank(b), lhsT=wv[:, kc, g * 128:(g + 1) * 128], rhs=qT[:, kc, 0:512],
                                       start=(kc == 0), stop=(kc == 7))
                    return ins
                A("pe", fn, r=xn_main + [B_ring[s_u]], w=[BK[b]])
                A("act", lambda e, g=g, b=b: e.activation(out=u_ext[:, g, 16:528], in_=bank(b), func=AF.Copy),
                  r=[BK[b]], w=[B_u])
            if t == 0:
                b = banks.one()

                def fn(e, b=b, wv=wv):
                    ins = None
                    for g in range(4):
                        for kc in range(8):
                            ins = e.matmul(bank(b)[:, g * 16:(g + 1) * 16], lhsT=wv[:, kc, g * 128:(g + 1) * 128],
                                           rhs=xnT[:, kc, 624:640], start=(kc == 0), stop=(kc == 7))
                    return ins
                A("pe", fn, r=[B_xnT[4], B_ring[s_u]], w=[BK[b]])
                A("act", lambda e, b=b: e.activation(out=u_ext[:, :, 0:16], in_=bank(b)[:, 0:64].rearrange("p (g c) -> p g c", c=16),
                                                   func=AF.Copy), r=[BK[b]], w=[B_u])
            if has_s:
                b = banks.one()

                def fn(e, b=b, wv=wv):
                    ins = None
                    for kc in range(8):
                        ins = e.matmul(bank(b)[:NSMP, :], lhsT=qT[:, kc, 512:512 + NSMP], rhs=wv[:, kc, :],
                                       start=(kc == 0), stop=(kc == 7))
                    return ins
                A("pe", fn, r=[B_qT[4], B_ring[s_u]], w=[BK[b]])
                A("act", lambda e, b=b: e.activation(out=us_tm, in_=bank(b)[:NSMP, :], func=AF.Copy), r=[BK[b], B_smp], w=[SB["us"]])

            if STOP == 'u':
                return
            if STOP == 'w5':
                for sl in range(1, 4):
                    bq = banks.one()
                    A("pe", lambda e, sl=sl, bq=bq: e.matmul(bank(bq)[:, 0:512], lhsT=xnT[:, 0, 0:128], rhs=ring[:, sl, 0:512], start=True, stop=True),
                      r=[B_ring[sl], B_xnT[0]], w=[BK[bq]])
                return
            if STOP == 'w6':
                for sl in range(1, 4):
                    bq = banks.one()
                    def f6(e, sl=sl, bq=bq):
                        ins = None
                        for kc in range(8):
                            ins = e.matmul(bank(bq)[:, 0:512], lhsT=xnT[:, kc, 0:128], rhs=ring[:, sl, kc * 512:(kc + 1) * 512], start=(kc == 0), stop=(kc == 7))
                        return ins
                    A("pe", f6, r=[B_ring[sl], B_xnT[0]], w=[BK[bq]])
                return
            if STOP in ('w2', 'w4'):
                for sl in range(1, 2 if STOP == 'w2' else 4):
                    bq = banks.one()
                    A("pe", lambda e, sl=sl, bq=bq: e.matmul(bank(bq)[:, 0:16], lhsT=ring[:, sl, 0:128], rhs=xnT[:, 0, 0:16], start=True, stop=True),
                      r=[B_ring[sl], B_xnT[0]], w=[BK[bq]])
                return
            release(1)
            E = u_ext
            for g in range(4):
                w_ = 2 << g
                cur = None
                A("pool", lambda e, g=g: e.tensor_tensor(out=ptmp[:, 0, 1:528], in0=E[:, g, 1:528], in1=E[:, g, 0:527], op=ALU.add),
                  r=[B_u], w=[B_pt[0]])
                ci = 0
                sh = 2
                lo = 1
                while sh < w_:
                    lo2 = lo + sh
                    A("pool", lambda e, ci=ci, sh=sh, lo2=lo2: e.tensor_tensor(out=ptmp[:, 1 - ci, lo2:528], in0=ptmp[:, ci, lo2:528],
                                                                               in1=ptmp[:, ci, lo2 - sh:528 - sh], op=ALU.add),
                      r=[B_pt[ci]], w=[B_pt[1 - ci]])
                    ci = 1 - ci
                    lo = lo2
                    sh *= 2
                A("dve", lambda e, g=g, ci=ci, w_=w_: e.scalar_tensor_tensor(out=m_sb[:, g, 0:512], in0=ptmp[:, ci, 16:528], scalar=1.0 / w_,
                                                                              in1=E[:, g, 16:528], op0=ALU.mult, op1=ALU.subtract),
                  r=[B_pt[ci], B_u], w=[B_m[g]])
                if t == 0:
                    A("dve", lambda e, g=g, ci=ci: e.tensor_tensor(out=ropet[:, 0, 0:16], in0=ptmp[:, ci, 16:32], in1=rc16[:, g, :], op=ALU.mult),
                      r=[B_pt[ci], B_const], w=[B_rope])
                    A("dve", lambda e, g=g: e.tensor_tensor(out=m_sb[:, g, 0:16], in0=ropet[:, 0, 0:16], in1=E[:, g, 16:32], op=ALU.subtract),
                      r=[B_rope, B_u], w=[B_m[g]])
            if last:
                b = banks.one()

                def fn(e, b=b):
                    ins = None
                    for g in range(4):
                        ins = e.transpose(out=bank(b)[:16, g * 128:(g + 1) * 128], in_=u_ext[:, g, 512:528], identity=identf[:, :])
                    return ins
                A("pe", fn, r=[B_u, B_const], w=[BK[b]])
                A("act", lambda e, b=b: e.activation(out=ework[:16, 1, 0:512], in_=bank(b)[:16, :], func=AF.Copy), r=[BK[b]], w=[B_ew[1]])
                A("sp", lambda e: [e.dma_start(out=npool, in_=ework[:16, 1, 0:512])], r=[B_ew[1], B_out], dsem=ds_npool)
            if not last:
                A("pool", lambda e: e.tensor_copy(out=u_ext[:, :, 0:16], in_=u_ext[:, :, 512:528]), r=[], w=[B_u])

            if has_s:
                bS = banks.one()

                def fnS(e, bS=bS):
                    ins = None
                    selv = sel.rearrange("p (h g c) -> p h g c", h=2, g=4)
                    for g in range(4):
                        for hf in range(2):
                            ins = e.matmul(bank(bS)[:16, g * 128:(g + 1) * 128], lhsT=selv[0:120, hf, g, :], rhs=st2[0:120, hf, g * 128:(g + 1) * 128],
                                           start=(hf == 0), stop=(hf == 1))
                    return ins
                A("pe", fnS, r=[SB["sel"], SB["st2"], B_smp], w=[BK[bS]])
                A("dve", lambda e, bS=bS: e.tensor_tensor(out=ssum, in0=bank(bS)[:16, :], in1=us_tm, op=ALU.add), r=[BK[bS], SB["us"], B_smp], w=[SB["ssum"]])
                for g in range(4):
                    w_ = 2 << g
                    A("dve", lambda e, g=g, w_=w_: e.scalar_tensor_tensor(out=ms_bf[:, g * 128:(g + 1) * 128], in0=ssum[:, g * 128:(g + 1) * 128],
                                                                         scalar=1.0 / w_, in1=us_tm[:, g * 128:(g + 1) * 128],
                                                                         op0=ALU.mult, op1=ALU.subtract), r=[SB["ssum"], SB["us"]], w=[SB["ms"]])
                transpose_to(lambda c: ms_bf[:, c * 128:(c + 1) * 128], SB["ms"], NSMP, 4, m_sb[:, :, 512:528], B_m)
                ckb.extend([(cks_g, cvs_g, SB["cks"], SB["cvs"]), (cks_B, cvs_B, SB["cksB"], SB["cvsB"])])
                ld_cast(0)
                ld_cast(1)
                A("sp", lambda e: [e.dma_start(out=nps[:, 14, :], in_=us_tm),
                                   e.dma_start(out=nps[:, 0:14, :], in_=spool[:, 1:15, :])], r=[SB["us"], B_out], dsem=ds_nps, ndma=2)

            for gi in range(4):
                s_g = next_unit()
                wv_ = rview(s_g, 8, 512)
                for mc in range(4):
                    j = gi * 4 + mc
                    b = banks.pair()

                    def fn(e, b=b, mc=mc, wv_=wv_):
                        ins = None
                        for kc in range(8):
                            ins = e.matmul(pair(b)[:, 0:512], lhsT=wv_[:, kc, mc * 128:(mc + 1) * 128], rhs=qT[:, kc, 0:512],
                                           start=(kc == 0), stop=(kc == 7))
                        if has_s:
                            for kc in range(8):
                                ins = e.matmul(pair(b)[:, 512:528], lhsT=wv_[:, kc, mc * 128:(mc + 1) * 128], rhs=qT[:, kc, 512:528],
                                               start=(kc == 0), stop=(kc == 7))
                        return ins
                    A("pe", fn, r=xn_all + [B_ring[s_g]], w=[BK[b], BK[b + 1]])
                    A("act", lambda e, b=b, j=j: e.activation(out=sg[:, j, 0:ncol], in_=pair(b)[:, 0:ncol], func=AF.Sigmoid),
                      r=[BK[b], BK[b + 1]], w=[B_sg[j]])
                release(1)

            if STOP == 'mix':
                return
            if not (has_s or t == 0):
                while tail_ops:
                    tail_ops.pop(0)()
                if t + 1 < NT:
                    load_x_tile(t + 1)
            s_q0 = next_unit()
            s_q1 = next_unit()
            s_kv = next_unit()
            wq0, wq1, wkv = rview(s_q0, 8, 512), rview(s_q1, 8, 512), rview(s_kv, 8, 512)
            prev_defer = []
            for (kind, nt, xap, xbuf, col0, cb, trow) in blocks:
                defer = []
                cosb = cs_sb[:nt, 0, trow, :]
                sinb = cs_sb[:nt, 1, trow, :]
                if kind != "h":
                    b = banks.pair()

                    def fn(e, b=b, col0=col0, nt=nt):
                        ins = None
                        for hf, wv_ in ((0, wq0), (1, wq1)):
                            for kc in range(8):
                                ins = e.matmul(pair(b)[:nt, hf * 512:(hf + 1) * 512], lhsT=qT[:, kc, col0:col0 + nt], rhs=wv_[:, kc, :],
                                               start=(kc == 0), stop=(kc == 7))
                        return ins
                    A("pe", fn, r=[B_qT[cb], B_ring[s_q0], B_ring[s_q1]], w=[BK[b], BK[b + 1]])
                    if STOP == 'q0':
                        continue
                    if STOP in ('k0', 'k1', 'k2', 'k3'):
                        pass
                    ti = ntm()
                    def qcopy(e, b=b, ti=ti, nt=nt):
                        e.activation(out=tmst[:nt, ti, 0:512], in_=pair(b)[:nt, 0:512], func=AF.Copy)
                        return e.activation(out=tmst[:nt, ti, 512:1024], in_=pair(b)[:nt, 512:1024], func=AF.Copy)
                    A("act", qcopy, r=[BK[b], BK[b + 1]], w=[B_tm[ti]])
                    if STOP == 'q1':
                        continue
                    Pv = pair(b)[:nt, :].rearrange("p (h d) -> p h d", d=64)
                    Qv = tmst[:nt, ti, :].rearrange("p (h d) -> p h d", d=64)
                    cb_ = cosb.unsqueeze(1).broadcast_to([nt, 16, 8])
                    sb_ = sinb.unsqueeze(1).broadcast_to([nt, 16, 8])
                    rt = [ropet[:nt, k, :].rearrange("p (h d) -> p h d", d=8) for k in range(4)]

                    A("act", lambda e, Pv=Pv, nt=nt: e.activation(out=ropeq[:nt, :, :], in_=Pv[:, :, 0:16], func=AF.Copy),
                      r=[BK[b], BK[b + 1]], w=[B_ropeq])
                    Rq = ropeq[:nt, :, :]

                    def rope_fn(e, Rq=Rq, cb_=cb_, sb_=sb_, rt=rt):
                        e.tensor_tensor(out=rt[0], in0=Rq[:, :, 0:8], in1=cb_, op=ALU.mult)
                        e.tensor_tensor(out=rt[1], in0=Rq[:, :, 8:16], in1=sb_, op=ALU.mult)
                        e.tensor_tensor(out=rt[2], in0=Rq[:, :, 8:16], in1=cb_, op=ALU.mult)
                        return e.tensor_tensor(out=rt[3], in0=Rq[:, :, 0:8], in1=sb_, op=ALU.mult)
                    A("dve", rope_fn, r=[B_ropeq, B_const], w=[B_rope])

                    def rope_fn2(e, Qv=Qv, rt=rt):
                        e.tensor_tensor(out=Qv[:, :, 0:8], in0=rt[0], in1=rt[1], op=ALU.subtract)
                        return e.tensor_tensor(out=Qv[:, :, 8:16], in0=rt[2], in1=rt[3], op=ALU.add)
                    if STOP == 'q2a':
                        continue
                    A("dve", rope_fn2, r=[B_rope], w=[B_tm[ti]])
                    if STOP == 'q2':
                        continue
                    defer.append(lambda ti=ti, nt=nt, col0=col0, cb=cb: transpose_to(
                        lambda c, ti=ti, nt=nt: tmst[:nt, ti, c * 128:(c + 1) * 128], B_tm[ti], nt, 8, qT[:, :, col0:col0 + nt], [B_qT[cb]]))
                    if kind == "s":
                        defer.append(lambda ti=ti, nt=nt: transpose_to(
                            lambda c, ti=ti, nt=nt: tmst[:nt, ti, 64 + c * 128:64 + (c + 1) * 128], [B_tm[ti], B_smp], nt, 7, qsh[:, :, 0:nt], [SB["qsh"]]))
                if STOP in ('q3', 'q0', 'q1', 'q2', 'q2a'):
                    continue
                b = banks.one()

                XNt, XNb = (xnT, B_xnT) if kind == "h" else (qT, B_qT)

                def fnkv(e, b=b, col0=col0, nt=nt, XNt=XNt):
                    ins = None
                    for kc in range(8):
                        ins = e.matmul(bank(b)[:nt, :], lhsT=XNt[:, kc, col0:col0 + nt], rhs=wkv[:, kc, :], start=(kc == 0), stop=(kc == 7))
                    return ins
                A("pe", fnkv, r=[XNb[cb], B_ring[s_kv]], w=[BK[b]])
                kf_i[0] += 1
                kfi = kf_i[0] % 2
                A("dve", lambda e, b=b, kfi=kfi, nt=nt: e.tensor_copy(out=kf[:nt, kfi, :], in_=bank(b)[:nt, :]),
                  r=[BK[b]], w=[B_kf[kfi]])
                if STOP == 'k1':
                    continue
                Pk = bank(b)[:nt, 0:256].rearrange("p (h d) -> p h d", d=64)
                Kv = kf[:nt, kfi, 0:256].rearrange("p (h d) -> p h d", d=64)
                cb4 = cosb.unsqueeze(1).broadcast_to([nt, 4, 8])
                sb4 = sinb.unsqueeze(1).broadcast_to([nt, 4, 8])
                rtk = [ropet[:nt, k, 0:32].rearrange("p (h d) -> p h d", d=8) for k in range(4)]

                def ropek(e, Kv=Kv, cb4=cb4, sb4=sb4, rtk=rtk):
                    e.tensor_tensor(out=rtk[0], in0=Kv[:, :, 0:8], in1=cb4, op=ALU.mult)
                    e.tensor_tensor(out=rtk[1], in0=Kv[:, :, 8:16], in1=sb4, op=ALU.mult)
                    e.tensor_tensor(out=rtk[2], in0=Kv[:, :, 8:16], in1=cb4, op=ALU.mult)
                    return e.tensor_tensor(out=rtk[3], in0=Kv[:, :, 0:8], in1=sb4, op=ALU.mult)
                A("dve", ropek, r=[B_kf[kfi], B_const], w=[B_rope])

                def ropek2(e, Kv=Kv, rtk=rtk):
                    e.tensor_tensor(out=Kv[:, :, 0:8], in0=rtk[0], in1=rtk[1], op=ALU.subtract)
                    return e.tensor_tensor(out=Kv[:, :, 8:16], in0=rtk[2], in1=rtk[3], op=ALU.add)
                A("dve", ropek2, r=[B_rope], w=[B_kf[kfi]])
                if STOP == 'k2':
                    continue
                if kind == "s":
                    A("act", lambda e, kfi=kfi: e.activation(out=knew, in_=kf[:NSMP, kfi, :], func=AF.Copy), r=[B_kf[kfi], B_smp], w=[SB["knew"]])
                    A("sp", lambda e, kfi=kfi: [e.dma_start(out=nks[:, 127, :], in_=kf[:NSMP, kfi, 0:256]),
                                                e.dma_start(out=nvs[:, 127, :], in_=kf[:NSMP, kfi, 256:512])],
                      r=[B_kf[kfi], B_out], dsem=ds_nkvs, ndma=2)
                else:
                    slot_k = (col0 // 128 + 1) if kind == "p" else 0
                    kti = kfi
                    def kdup(e, kfi=kfi, kti=kti, nt=nt):
                        kd = k_tm[:nt, kti, :].rearrange("p (h a d) -> p h a d", a=2, d=64)
                        src = kf[:nt, kfi, 0:256].rearrange("p (h d) -> p h d", d=64)
                        e.tensor_copy(out=kd[:, :, 0, :], in_=src)
                        return e.tensor_copy(out=kd[:, :, 1, :], in_=src)
                    A("pool", kdup, r=[B_kf[kfi]], w=[B_ktm[kti]])
                    A("pool", lambda e, kfi=kfi, slot_k=slot_k, nt=nt: e.tensor_copy(out=vaug[:nt, slot_k, :, 0:64],
                                                                                   in_=kf[:nt, kfi, 256:512].rearrange("p (h d) -> p h d", d=64)),
                      r=[B_kf[kfi]], w=[B_v[slot_k]])
                    if STOP == 'k3':
                        continue
                    defer.append(lambda kti=kti, nt=nt, slot_k=slot_k: transpose_to(
                        lambda c, kti=kti, nt=nt: k_tm[:nt, kti, c * 128:(c + 1) * 128], B_ktm[kti], nt, 4,
                        kT[:, :, slot_k * 128:slot_k * 128 + nt], [B_kT[slot_k]]))
                    if last and kind == "p" and col0 == 384:
                        A("sp", lambda e, kfi=kfi: [e.dma_start(out=nk, in_=kf[:, kfi, 0:256]), e.dma_start(out=nv, in_=kf[:, kfi, 256:512])],
                          r=[B_kf[kfi], B_out], dsem=ds_misc, ndma=2)
                for f_ in prev_defer:
                    f_()
                prev_defer = defer

            for f_ in prev_defer:
                f_()
            if STOP in ('qkv', 'q0', 'q1', 'q2', 'q2a', 'q3', 'k1', 'k2', 'k3'):
                return
            release(3)
            if STOP == 'gates':
                return
            if t == 0:
                A("act", lambda e: e.activation(out=es_sb[:], in_=es_sb[:], func=AF.Exp), r=[], w=[B_const])
            mk = 0 if t == 0 else 1
            segs = [(0, 0, 0, 128), (1, 128, 0, 256), (2, 384, 128, 128), (2, 512, 256, 128), (3, 640, 256, 256), (4, 896, 384, 128)]

            def emit_scores(h):
                cq, po, kv = h // 2, (h % 2) * 64, h // 4
                b = banks.pair()

                def fn(e, b=b, po=po, kv=kv, cq=cq):
                    ins = None
                    for hb in range(2):
                        ins = e.matmul(pair(b)[:, hb * 512:(hb + 1) * 512], lhsT=ident[:, :], rhs=masks[:, mk, hb * 512:(hb + 1) * 512], start=True, stop=False)
                    for si_, (s_, c0, q0, n_) in enumerate(segs):
                        ins = e.matmul(pair(b)[:, c0:c0 + n_], lhsT=kT[po:po + 64, kv, s_ * 128:(s_ + 1) * 128], rhs=qT[po:po + 64, cq, q0:q0 + n_],
                                       start=False, stop=(si_ in (2, 5)))
                    return ins
                A("pe", fn, r=B_qT[0:4] + B_kT + [B_const], w=[BK[b], BK[b + 1]])
                pi = h % 2
                A("act", lambda e, b=b, pi=pi: e.activation(out=PT[:, pi, :], in_=pair(b)[:, :], func=AF.Exp, scale=0.125),
                  r=[BK[b], BK[b + 1]], w=[B_PT[pi]])

            def emit_pv(h):
                kv = h // 4
                pi = h % 2
                ob = banks.one()

                def fnpv(e, ob=ob, pi=pi, kv=kv):
                    ins = None
                    for qb in range(4):
                        for wh in range(2):
                            ins = e.matmul(bank(ob)[:, qb * 65:qb * 65 + 65], lhsT=PT[:, pi, (2 * qb + wh) * 128:(2 * qb + wh + 1) * 128],
                                           rhs=vaug[:, qb + wh, kv, :], start=(wh == 0), stop=(wh == 1))
                    return ins
                A("pe", fnpv, r=[B_PT[pi]] + B_v, w=[BK[ob]])
                Ov = bank(ob)[:, 0:260].rearrange("p (q d) -> p q d", d=65)
                di = h % 2
                A("act", lambda e, Ov=Ov, di=di, h=h: e.activation(out=dn[:, di, 0:4], in_=Ov[:, :, 64], func=AF.Identity, bias=es_sb[:, h:h + 1]),
                  r=[BK[ob], B_const], w=[B_dn[di]])
                A("act", lambda e, Ov=Ov, di=di: e.activation(out=Osb[:, di, :].rearrange("p (q d) -> p q d", d=64), in_=Ov[:, :, 0:64], func=AF.Copy),
                  r=[BK[ob]], w=[B_Osb[di]])
                A("dve", lambda e, di=di: e.reciprocal(out=dn[:, di, 4:8], in_=dn[:, di, 0:4]), r=[], w=[B_dn[di]])
                A("dve", lambda e, di=di, h=h: e.tensor_tensor(out=attn_tm[:, :, h * 64:(h + 1) * 64], in0=Osb[:, di, :].rearrange("p (q d) -> p q d", d=64),
                                                               in1=dn[:, di, 4:8].unsqueeze(2).broadcast_to([128, 4, 64]), op=ALU.mult),
                  r=[B_Osb[di], B_dn[di]], w=[B_ew[0], B_ew[1]])
            pend = None
            for h in range(16):
                emit_scores(h)
                if pend is not None:
                    emit_pv(pend)
                pend = h
            emit_pv(pend)
            for qb in range(4):
                transpose_to(lambda c, qb=qb: attn_tm[:, qb, c * 128:(c + 1) * 128], [B_ew[0], B_ew[1]], 128, 8, xnT[:, :, qb * 128:(qb + 1) * 128], [B_xnT[qb]])
            if not last:
                A("pool", lambda e: e.tensor_copy(out=kT[:, :, 0:128], in_=kT[:, :, 512:640]), r=[B_kT[4]], w=[B_kT[0]])
                A("pool", lambda e: e.tensor_copy(out=vaug[:, 0, :, 0:64], in_=vaug[:, 4, :, 0:64]), r=[B_v[4]], w=[B_v[0]])

            if STOP == 'attn' and last:
                return
            for g in range(4):
                b = banks.pair()
                A("pe", lambda e, g=g, b=b: e.matmul(pair(b)[:, 0:512], lhsT=Gw[:, g, :], rhs=m_sb[:, g, 0:512], start=True, stop=True)
                  if not has_s else
                  (e.matmul(pair(b)[:, 0:512], lhsT=Gw[:, g, :], rhs=m_sb[:, g, 0:512], start=True, stop=True),
                   e.matmul(pair(b)[:, 512:528], lhsT=Gw[:, g, :], rhs=m_sb[:, g, 512:528], start=True, stop=True))[1],
                  r=[B_m[g], B_const], w=[BK[b], BK[b + 1]])
                A("dve", lambda e, g=g, b=b: e.tensor_scalar(out=mixT[:, g, 0:ncol], in0=pair(b)[:, 0:ncol], scalar1=pscale[:, g:g + 1], scalar2=None,
                                                             op0=ALU.mult), r=[BK[b], BK[b + 1], B_const], w=[B_mix[g]])

            if STOP == 'attn2':
                return
            s_pb = next_unit()
            s_ab0 = next_unit()
            s_ab1 = next_unit()
            wpbv = rview(s_pb, 4, 1024)
            wabv = [rview(s_ab0, 8, 512), rview(s_ab1, 8, 512)]
            for mc in range(8):
                ba = banks.pair()

                def fna(e, ba=ba, mc=mc):
                    ins = None
                    for kc in range(4):
                        ins = e.matmul(pair(ba)[:, 0:512], lhsT=wpbv[:, kc, mc * 128:(mc + 1) * 128], rhs=mixT[:, kc, 0:512], start=(kc == 0), stop=(kc == 3))
                    return ins
                A("pe", fna, r=B_mix + [B_ring[s_pb]], w=[BK[ba], BK[ba + 1]])
                bb = banks.pair()
                wv_ = wabv[mc // 4]

                def fnb(e, bb=bb, mc=mc, wv_=wv_):
                    ins = None
                    mcl = mc % 4
                    for kc in range(8):
                        ins = e.matmul(pair(bb)[:, 0:512], lhsT=wv_[:, kc, mcl * 128:(mcl + 1) * 128], rhs=xnT[:, kc, 0:512], start=(kc == 0), stop=(kc == 7))
                    return ins
                A("pe", fnb, r=B_xnT[0:4] + [B_ring[s_ab0], B_ring[s_ab1]], w=[BK[bb], BK[bb + 1]])
                A("dve", lambda e, ba=ba, mc=mc: e.tensor_tensor(out=tmpf[:, 0, 0:512], in0=pair(ba)[:, 0:512], in1=sg[:, mc, 0:512], op=ALU.mult),
                  r=[BK[ba], BK[ba + 1], B_sg[mc]], w=[B_tmpf[0]])
                A("dve", lambda e, bb=bb, mc=mc: e.tensor_tensor(out=tmpf[:, 1, 0:512], in0=pair(bb)[:, 0:512], in1=sg[:, 8 + mc, 0:512], op=ALU.mult),
                  r=[BK[bb], BK[bb + 1], B_sg[8 + mc]], w=[B_tmpf[1]])
                A("dve", lambda e, mc=mc: e.tensor_tensor(out=mrgT[:, mc, 0:512], in0=tmpf[:, 0, 0:512], in1=tmpf[:, 1, 0:512], op=ALU.add),
                  r=[B_tmpf[0], B_tmpf[1]], w=[B_mrg[mc]])
            if has_s:
                for h in range(16):
                    kv, po = h // 4, ((h // 4) % 2) * 64
                    if (h % 2) * 64 == po:
                        src = qT[po:po + 64, h // 2, 512:528]
                        rb = [B_qT[4]]
                    else:
                        cpr = (h - 1) // 2 if po == 0 else (h - 2) // 2
                        src = qsh[po:po + 64, cpr, :]
                        rb = [SB["qsh"]]
                    A("act", lambda e, src=src, po=po, h=h: e.activation(out=qsel[po:po + 64, :, h], in_=src, func=AF.Copy), r=rb + [B_smp], w=[SB["qsel"]])
                bPS = banks.one()
                bOT = banks.one()
                banks.reserved = (bPS, bOT)
                OTv = bank(bOT)[:, 0:128].rearrange("p (c s) -> p c s", s=16)
                PTv = PTs.rearrange("p (sk pr two) -> p sk pr two", pr=2, two=2)
                ld_row(0)
                ld_row(1)
                for g4 in range(4):
                    cks_c, cvs_c, bk_c, bv_c = ckb[g4 % 2]
                    transpose_to(lambda c, cks_c=cks_c: cks_c[:, c % 4, (c // 4) * 128:(c // 4 + 1) * 128], bk_c, 128, 8, ksT_g, [SB["ksT"]])

                    def fsc(e, g4=g4):
                        ins = None
                        for sl in range(4):
                            s_ = g4 * 4 + sl
                            for kv in range(4):
                                po = (kv % 2) * 64
                                ins = e.matmul(bank(bPS)[:, s_ * 16 + kv * 4:s_ * 16 + kv * 4 + 4], lhsT=ksT_g[po:po + 64, (kv // 2) * 4 + sl, :],
                                               rhs=qsel[po:po + 64, s_, kv * 4:kv * 4 + 4], start=True, stop=True)
                        return ins
                    A("pe", fsc, r=[SB["ksT"], SB["qsel"]], w=[BK[bPS]])
                    A("act", lambda e, g4=g4: e.activation(out=PTs[:, g4 * 64:(g4 + 1) * 64], in_=bank(bPS)[:, g4 * 64:(g4 + 1) * 64], func=AF.Exp, scale=0.125),
                      r=[BK[bPS]], w=[SB["PTs"]])

                    def fpv(e, g4=g4, cvs_c=cvs_c):
                        ins = None
                        for sl in range(4):
                            s_ = g4 * 4 + sl
                            for kv in range(4):
                                for par in range(2):
                                    ins = e.matmul(OTv[par * 64:(par + 1) * 64, 2 * kv:2 * kv + 2, s_], lhsT=cvs_c[:, sl, kv * 64:(kv + 1) * 64],
                                                   rhs=PTv[:, s_ * 4 + kv, :, par], start=True, stop=True)
                        return ins
                    A("pe", fpv, r=[SB["PTs"], bv_c], w=[BK[bOT]])
                    if g4 + 2 < 4:
                        ld_cast(g4 + 2)
                        ld_row(g4 + 2)
                bD = banks.one()
                A("pe", lambda e: e.matmul(bank(bD)[:, 0:256], lhsT=ones_bf[:, :], rhs=PTs, start=True, stop=True), r=[SB["PTs"], B_const], w=[BK[bD]])
                A("act", lambda e: e.activation(out=rds, in_=bank(bD)[:, 0:256], func=AF.Copy), r=[BK[bD]], w=[SB["rds"]])
                A("dve", lambda e: e.tensor_tensor(out=rds.rearrange("p (s h) -> p s h", h=16), in0=rds.rearrange("p (s h) -> p s h", h=16),
                                                   in1=es_sb[:, 0:16].unsqueeze(1).broadcast_to([128, 16, 16]), op=ALU.add), r=[B_const], w=[SB["rds"]])
                A("dve", lambda e: e.reciprocal(out=rds, in_=rds), w=[SB["rds"]])
                A("act", lambda e: e.activation(out=OTsb, in_=bank(bOT)[:, 0:128], func=AF.Copy), r=[BK[bOT]], w=[SB["OTs"]])
                rdv = rds.rearrange("p (s c two) -> p c s two", c=8, two=2)
                OSv = OTsb.rearrange("p (c s) -> p c s", s=16)

                def fnn(e):
                    e.tensor_tensor(out=xnT[0:64, :, 512:528], in0=OSv[0:64], in1=rdv[0:64, :, :, 0], op=ALU.mult)
                    return e.tensor_tensor(out=xnT[64:128, :, 512:528], in0=OSv[64:128], in1=rdv[64:128, :, :, 1], op=ALU.mult)
                A("dve", fnn, r=[SB["OTs"], SB["rds"]], w=[B_xnT[4]])
                banks.reserved = ()
                A("pool", lambda e: e.memset(dummy[:, 1:2], 0.0), w=[B_smp] + list(SB.values()) + B_x[0:4] + [B_gate] + B_pT)

            if has_s:
                bsa = banks.one()
                bsb = banks.one()

                def fms(e, bsa=bsa, bsb=bsb):
                    ins = None
                    for mc in range(8):
                        for kc in range(4):
                            ins = e.matmul(bank(bsa)[:, mc * 16:(mc + 1) * 16], lhsT=wpbv[:, kc, mc * 128:(mc + 1) * 128], rhs=mixT[:, kc, 512:528],
                                           start=(kc == 0), stop=(kc == 3))
                    for mc in range(8):
                        wv_ = wabv[mc // 4]
                        mcl = mc % 4
                        for kc in range(8):
                            ins = e.matmul(bank(bsb)[:, mc * 16:(mc + 1) * 16], lhsT=wv_[:, kc, mcl * 128:(mcl + 1) * 128], rhs=xnT[:, kc, 512:528],
                                           start=(kc == 0), stop=(kc == 7))
                    return ins
                A("pe", fms, r=B_mix + [B_xnT[4], B_ring[s_pb], B_ring[s_ab0], B_ring[s_ab1]], w=[BK[bsa], BK[bsb]])
                tA = tmpf[:, 0, 0:128].rearrange("p (m c) -> p m c", c=16)
                tB = tmpf[:, 1, 0:128].rearrange("p (m c) -> p m c", c=16)
                A("dve", lambda e, bsa=bsa: e.tensor_tensor(out=tA, in0=bank(bsa)[:, 0:128].rearrange("p (m c) -> p m c", c=16), in1=sg[:, 0:8, 512:528], op=ALU.mult),
                  r=[BK[bsa]] + B_sg[0:8], w=[B_tmpf[0]])
                A("dve", lambda e, bsb=bsb: e.tensor_tensor(out=tB, in0=bank(bsb)[:, 0:128].rearrange("p (m c) -> p m c", c=16), in1=sg[:, 8:16, 512:528], op=ALU.mult),
                  r=[BK[bsb]] + B_sg[8:16], w=[B_tmpf[1]])
                A("dve", lambda e: e.tensor_tensor(out=mrgT[:, :, 512:528], in0=tA, in1=tB, op=ALU.add), r=[B_tmpf[0], B_tmpf[1]], w=B_mrg)

            if STOP == 'merge':
                return
            release(3)
            tblocks = [bl for bl in blocks if bl[0] != "h"]

            s_o = [next_unit(), next_unit()]
            wvo = [rview(s_o[0], 8, 512), rview(s_o[1], 8, 512)]
            for (kind, nt, xap, xbuf, col0, cb, trow) in tblocks:
                for hf in range(2):
                    b = banks.one()

                    def fn(e, b=b, col0=col0, nt=nt, hf=hf):
                        ins = None
                        for kc in range(8):
                            ins = e.matmul(bank(b)[:nt, :], lhsT=mrgT[:, kc, col0:col0 + nt], rhs=wvo[hf][:, kc, :], start=(kc == 0), stop=(kc == 7))
                        return ins
                    A("pe", fn, r=B_mrg + [B_ring[s_o[hf]]], w=[BK[b]])
                    A("dve", lambda e, b=b, xap=xap, hf=hf, nt=nt: e.tensor_tensor(out=xap[:, hf * 512:(hf + 1) * 512], in0=bank(b)[:nt, :],
                                                                                  in1=xap[:, hf * 512:(hf + 1) * 512], op=ALU.add),
                      r=[BK[b]], w=[xbuf])
            release(2)
            if STOP == 'wout':
                return
            for grp in (tblocks[0:4], tblocks[4:]):
                tis = []
                for (kind, nt, xap, xbuf, col0, cb, trow) in grp:
                    ti = ntm()
                    tis.append(ti)
                    rmsnorm_to_bf16(xap, xbuf, nt, 1, tmst[:nt, ti, :], B_tm[ti])
                for ti, (kind, nt, xap, xbuf, col0, cb, trow) in zip(tis, grp):
                    transpose_to(lambda c, ti=ti, nt=nt: tmst[:nt, ti, c * 128:(c + 1) * 128], B_tm[ti], nt, 8,
                                 xnT[:, :, col0:col0 + nt], [B_xnT[cb]])
            if STOP == 'ln2':
                return
            if t + 1 < NT:
                nslot = (t + 1) % 2
                tis_n = []
                for b_ in range(4):
                    ti = ntm()
                    tis_n.append(ti)
                    rmsnorm_to_bf16(x_tm[:, nslot * 4 + b_, :], B_x[nslot * 4 + b_], 128, 0, tmst[:, ti, :], B_tm[ti])
                pre_tis[t + 1] = tis_n
            def actT(hc):
                return sg[:, hc, :] if hc < 16 else qT[:, hc - 16, :]

            def actB(hc):
                return [B_sg[hc]] if hc < 16 else list(B_qT)
            for n in range(11):
                s_f = next_unit()
                wv_ = rview(s_f, 8, 512)
                for l in range(2):
                    hc = 2 * n + l
                    bg = banks.pair()
                    bu = banks.pair()

                    def fn(e, bg=bg, bu=bu, l=l, wv_=wv_):
                        ins = None
                        for (bb_, coff) in ((bg, l * 128), (bu, 256 + l * 128)):
                            for kc in range(8):
                                ins = e.matmul(pair(bb_)[:, 0:512], lhsT=wv_[:, kc, coff:coff + 128], rhs=xnT[:, kc, 0:512], start=(kc == 0), stop=(kc == 7))
                            if has_s:
                                for kc in range(8):
                                    ins = e.matmul(pair(bb_)[:, 512:528], lhsT=wv_[:, kc, coff:coff + 128], rhs=xnT[:, kc, 512:528],
                                                   start=(kc == 0), stop=(kc == 7))
                        return ins
                    A("pe", fn, r=hn_all + [B_ring[s_f]], w=[BK[bg], BK[bg + 1], BK[bu], BK[bu + 1]])
                    tf = hc % 2
                    A("act", lambda e, bg=bg, tf=tf: e.activation(out=tmpf[:, tf, 0:ncol], in_=pair(bg)[:, 0:ncol], func=AF.Silu),
                      r=[BK[bg], BK[bg + 1]], w=[B_tmpf[tf]])
                    A("dve", lambda e, bu=bu, tf=tf, hc=hc: e.tensor_tensor(out=actT(hc)[:, 0:ncol], in0=pair(bu)[:, 0:ncol], in1=tmpf[:, tf, 0:ncol], op=ALU.mult),
                      r=[BK[bu], BK[bu + 1], B_tmpf[tf]], w=actB(hc))
                release(1)
            if STOP == 'ffi':
                return
            allact = list(B_sg) + list(B_qT)
            for hf in range(2):
                bks = [banks.one() for _ in tblocks]
                for ui, k0 in enumerate((0, 8, 16)):
                    s_f = next_unit()
                    nkc = min(8, NHC - k0)
                    wv_ = rview(s_f, nkc, 512)
                    for bi, (kind, nt, xap, xbuf, col0, cb, trow) in enumerate(tblocks):
                        b = bks[bi]

                        def fn(e, b=b, col0=col0, nt=nt, wv_=wv_, k0=k0, nkc=nkc):
                            ins = None
                            for kk in range(nkc):
                                ins = e.matmul(bank(b)[:nt, :], lhsT=actT(k0 + kk)[:, col0:col0 + nt], rhs=wv_[:, kk, :],
                                               start=(k0 + kk == 0), stop=(k0 + kk == NHC - 1))
                            return ins
                        A("pe", fn, r=allact + [B_ring[s_f]], w=[BK[b]])
                    release(1)
                for bi, (kind, nt, xap, xbuf, col0, cb, trow) in enumerate(tblocks):
                    b = bks[bi]
                    A("dve", lambda e, b=b, xap=xap, hf=hf, nt=nt: e.tensor_tensor(out=xap[:, hf * 512:(hf + 1) * 512], in0=bank(b)[:nt, :],
                                                                                  in1=xap[:, hf * 512:(hf + 1) * 512], op=ALU.add),
                      r=[BK[b]], w=[xbuf])
            if STOP == 'ffo':
                return
            if t + 1 < NT and (t + 1) in pre_tis:
                for b_, ti in enumerate(pre_tis[t + 1]):
                    transpose_to(lambda c, ti=ti: tmst[:, ti, c * 128:(c + 1) * 128], B_tm[ti], 128, 8, qT[:, :, b_ * 128:(b_ + 1) * 128], [B_qT[b_]])
                done_TX[t + 1] = True
                if DO_SAMPLE and t + 1 == NT - 1:
                    ti = ntm()
                    rmsnorm_to_bf16(x_aux[:NSMP, :], B_xaux, NSMP, 0, tmst[:NSMP, ti, :], B_tm[ti])
                    transpose_to(lambda c, ti=ti: tmst[:NSMP, ti, c * 128:(c + 1) * 128], B_tm[ti], NSMP, 8, qT[:, :, 512:512 + NSMP], [B_qT[4]])
                    done_aux[t + 1] = True
            for bi, (kind, nt, xap, xbuf, col0, cb, trow) in enumerate(tblocks):
                transpose_f32_to(lambda c, xap=xap: xap[:, c * 128:(c + 1) * 128], [xbuf], nt, 8, xnT[:, :, col0:col0 + nt], [B_xnT[cb]])
                pidx = bi if kind == "p" else 4
                transpose_f32_to(lambda c, pidx=pidx, nt=nt: p_tm[:nt, pidx, c * 128:(c + 1) * 128], [B_p[pidx]], nt, 2,
                                 pT[:, :, col0:col0 + nt], [B_pT[cb]])
            s_g0 = next_unit()
            s_g1 = next_unit()
            s_pp = next_unit()
            wg = [rview(s_g0, 8, 512), rview(s_g1, 8, 512)]
            wppv = rview(s_pp, 2, 1024)
            for bi, (kind, nt, xap, xbuf, col0, cb, trow) in enumerate(tblocks):
                bgp = banks.pair()

                def fng(e, bgp=bgp, col0=col0, nt=nt):
                    ins = None
                    for hf in range(2):
                        for kc in range(8):
                            ins = e.matmul(pair(bgp)[:nt, hf * 512:(hf + 1) * 512], lhsT=xnT[:, kc, col0:col0 + nt], rhs=wg[hf][:, kc, :],
                                           start=(kc == 0), stop=(kc == 7))
                    return ins
                A("pe", fng, r=[B_xnT[cb], B_ring[s_g0], B_ring[s_g1]], w=[BK[bgp], BK[bgp + 1]])
                if bi % 2 == 0:
                    gt_ap, gt_b = gate_tm[:nt, :], [B_gate]
                else:
                    gt_ap, gt_b = Osb[:nt].rearrange("p a d -> p (a d)").bitcast(BF16), [B_Osb[0], B_Osb[1]]
                A("act", lambda e, bgp=bgp, nt=nt, gt_ap=gt_ap: e.activation(out=gt_ap, in_=pair(bgp)[:nt, :], func=AF.Sigmoid),
                  r=[BK[bgp], BK[bgp + 1]], w=gt_b)
                bep = banks.pair()

                def fne(e, bep=bep, col0=col0, nt=nt):
                    ins = None
                    for hf in range(2):
                        for kc in range(2):
                            ins = e.matmul(pair(bep)[:nt, hf * 512:(hf + 1) * 512], lhsT=pT[:, kc, col0:col0 + nt], rhs=wppv[:, kc, hf * 512:(hf + 1) * 512],
                                           start=(kc == 0), stop=(kc == 1))
                    return ins
                A("pe", fne, r=[B_pT[cb], B_ring[s_pp]], w=[BK[bep], BK[bep + 1]])
                ei = bi % 2
                ew = ework[:nt, ei, :]
                def ecopy(e, bep=bep, ew=ew, nt=nt):
                    e.tensor_copy(out=ew[:, 0:512], in_=pair(bep)[:nt, 0:512])
                    return e.tensor_copy(out=ew[:, 512:1024], in_=pair(bep)[:nt, 512:1024])
                A("dve", ecopy, r=[BK[bep], BK[bep + 1]], w=[B_ew[ei]])
                si = nscal()
                c = si * 3
                A("act", lambda e, ew=ew, c=c, nt=nt, bep=bep: e.activation(out=pair(bep)[:nt, :], in_=ew, func=AF.Square, accum_out=scal[:nt, c:c + 1]),
                  r=[B_ew[ei]], w=[BK[bep], BK[bep + 1], B_scal[si]])
                A("dve", lambda e, c=c, nt=nt: e.tensor_scalar(out=scal[:nt, c + 1:c + 2], in0=scal[:nt, c:c + 1], scalar1=1.0 / D, scalar2=EPS, op0=ALU.mult, op1=ALU.add),
                  w=[B_scal[si]])
                A("pool", lambda e, c=c, nt=nt: e.tensor_tensor(out=scal[:nt, c + 2:c + 3], in0=scal[:nt, c + 1:c + 2], in1=nhalf[:nt, :], op=ALU.pow), r=[B_ca], w=[B_scal[si]])
                A("dve", lambda e, ew=ew, c=c, nt=nt: e.scalar_tensor_tensor(out=ew, in0=ew, scalar=scal[:nt, c + 2:c + 3], in1=gains[:nt, 2, :],
                                                                            op0=ALU.mult, op1=ALU.mult), r=[B_scal[si], B_const], w=[B_ew[ei]])
                A("dve", lambda e, ew=ew, nt=nt, gt_ap=gt_ap: e.tensor_tensor(out=ew, in0=ew, in1=gt_ap, op=ALU.mult), r=gt_b, w=[B_ew[ei]])
                A("dve", lambda e, ew=ew, xap=xap: e.tensor_tensor(out=xap, in0=xap, in1=ew, op=ALU.add), r=[B_ew[ei]], w=[xbuf])
                def tail_fn(kind=kind, nt=nt, xap=xap, xbuf=xbuf, bi=bi, ew=ew, ei=ei, t=t):
                    si = nscal()
                    c = si * 3
                    yo = xap
                    A("act", lambda e, xap=xap, c=c, nt=nt, ew=ew: e.activation(out=ew, in_=xap, func=AF.Square, accum_out=scal[:nt, c:c + 1]),
                      r=[xbuf], w=[B_ew[ei], B_scal[si]])
                    A("dve", lambda e, c=c, nt=nt: e.tensor_scalar(out=scal[:nt, c + 1:c + 2], in0=scal[:nt, c:c + 1], scalar1=1.0 / D, scalar2=EPS, op0=ALU.mult, op1=ALU.add),
                      w=[B_scal[si]])
                    A("pool", lambda e, c=c, nt=nt: e.tensor_tensor(out=scal[:nt, c + 2:c + 3], in0=scal[:nt, c + 1:c + 2], in1=nhalf[:nt, :], op=ALU.pow), r=[B_ca], w=[B_scal[si]])
                    A("dve", lambda e, yo=yo, xap=xap, c=c, nt=nt: e.scalar_tensor_tensor(out=yo, in0=xap, scalar=scal[:nt, c + 2:c + 3], in1=gains[:nt, 3, :],
                                                                                         op0=ALU.mult, op1=ALU.mult), r=[B_scal[si], B_const], w=[xbuf])
                    if kind == "p":
                        r0 = t * 512 + bi * 128
                        A("sp", lambda e, yo=yo, r0=r0: [e.dma_start(out=y[r0:r0 + 128, :], in_=yo)], r=[xbuf, B_out], dsem=ds_yo[(t % 2) * 4 + bi])
                    else:
                        A("sp", lambda e, yo=yo: [e.dma_start(out=ys, in_=yo)], r=[xbuf, B_out], dsem=ds_yo[8])
                if last:
                    tail_fn()
                else:
                    tail_ops.append(tail_fn)
            release(3)

        for t_ in range(NT):
            do_tile(t_)
        while tail_ops:
            tail_ops.pop(0)()

        if DO_SAMPLE:
            A("sp", lambda e: [e.dma_start(out=nks[:, 0:127, :], in_=ck[:, 1:128, :]), e.dma_start(out=nvs[:, 0:127, :], in_=cv[:, 1:128, :])],
              r=[B_out], dsem=ds_smp, ndma=2)
        A("sp", lambda e: None, r=B_ring + B_x + B_p + [B_xaux, B_const] + B_scr, w=[B_out])

        S.finalize(sems)
        block = es.enter_context(nc.Block())

        @block.tensor
        def _(e):
            S.emit("pe", e)

        @block.scalar
        def _(e):
            S.emit("act", e)

        @block.vector
        def _(e):
            S.emit("dve", e)

        @block.gpsimd
        def _(e):
            S.emit("pool", e)

        @block.sync
        def _(e):
            S.emit("sp", e)
    return nc


def sample_attention(L):
    raise NotImplementedError


_PROG = None


def _tables():
    half = 8
    inv = (500000.0 ** (-(np.arange(0, 16, 2, dtype=np.float32) / np.float32(16)))).astype(np.float32)
    return inv


def kernel(**inp):
    global _PROG
    bf = ml_dtypes.bfloat16
    x_prompt = np.asarray(inp["x_prompt"], np.float32)
    x_sample = np.asarray(inp["x_sample"], np.float32)
    p_prompt = np.asarray(inp["p_prompt"], np.float32)
    p_sample = np.asarray(inp["p_sample"], np.float32)
    cache_k = np.asarray(inp["cache_k"], np.float32)
    cache_v = np.asarray(inp["cache_v"], np.float32)
    state_pool = np.asarray(inp["state_pool"], np.float32)
    if _PROG is None:
        _PROG = build_program()
    nc = _PROG
    inv = _tables()
    kk = np.arange(128)[:, None]
    qq = np.arange(128)[None, :]
    m_prev = np.where(kk > qq, 0.0, -30000.0).astype(np.float32)
    m_diag = np.where(kk <= qq, 0.0, -30000.0).astype(np.float32)
    mask1 = np.concatenate([m_prev, m_diag] * 4, axis=1)
    ident = np.eye(128, dtype=np.float32).astype(bf)
    shared = {}
    for k_ in ("ln1", "w_in", "pool_group_w", "pool_scale", "attn_sinks", "w_pool_branch", "w_attn_branch", "w_out", "ln2",
               "w_ffn_in", "w_ffn_out", "w_ple_proj", "ple_norm", "w_ple_gate"):
        shared[k_] = np.ascontiguousarray(np.asarray(inp[k_], np.float32)[0])
    shared["final_norm"] = np.ascontiguousarray(np.asarray(inp["final_norm"], np.float32))
    shared["ident"] = ident
    shared["identf"] = np.eye(128, dtype=np.float32)
    selm = np.zeros((128, 2, 4, 16), np.float32)
    for hf in range(2):
        for s8 in range(8):
            for r_ in range(15):
                for g in range(4):
                    if r_ >= 16 - (2 << g):
                        selm[s8 * 15 + r_, hf, g, hf * 8 + s8] = 1.0
    shared["sel"] = selm.reshape(128, 128)
    in_maps = []
    for c in range(NCORE):
        bi, j = c // 4, c % 4
        t0 = j * TOKC
        xh = np.zeros((TOKC + 128, D), np.float32)
        xh[128:] = x_prompt[bi, t0:t0 + TOKC]
        if j > 0:
            xh[:128] = x_prompt[bi, t0 - 128:t0]
        pos = np.concatenate([np.arange(t0 - 128, t0 + TOKC), np.full(128, 16384)]).astype(np.float32)
        ang = pos[:, None] * inv[None, :]
        cost = np.cos(ang).astype(np.float32)
        sint = np.sin(ang).astype(np.float32)
        mask0 = mask1.copy()
        if j == 0:
            mask0[:, 0:128] = -30000.0
        maskd = np.stack([mask0, mask1], axis=1).astype(bf)
        rc = np.zeros((4, 16), np.float32)
        for g in range(4):
            w_ = 2 << g
            for tt in range(16):
                rc[g, tt] = 1.0 / (min(w_, tt + 1) if j == 0 else w_)
        rc16 = np.ascontiguousarray(np.broadcast_to(rc.reshape(1, 64), (128, 64)))
        m = dict(shared)
        m.update({
            "xh": xh, "ph": np.ascontiguousarray(p_prompt[0, bi, t0:t0 + TOKC]),
            "xs": np.ascontiguousarray(x_sample[c * NSMP:(c + 1) * NSMP, 0]),
            "ps": np.ascontiguousarray(p_sample[0, c * NSMP:(c + 1) * NSMP, 0]),
            "ck": np.ascontiguousarray(cache_k[0, c * NSMP:(c + 1) * NSMP].reshape(NSMP, 128, 256)),
            "cv": np.ascontiguousarray(cache_v[0, c * NSMP:(c + 1) * NSMP].reshape(NSMP, 128, 256)),
            "spool": np.ascontiguousarray(state_pool[0, c * NSMP:(c + 1) * NSMP]),
            "cs": np.ascontiguousarray(np.stack([cost.reshape(18, 128, 8), sint.reshape(18, 128, 8)], axis=0).transpose(2, 0, 1, 3).reshape(128, 288)),
            "maskd": maskd, "rc16": rc16,
        })
        in_maps.append(m)
    res = run_bass_kernel_spmd(nc, in_maps, core_ids=list(range(NCORE)))
    R = res.results
    y_prompt = np.stack([np.concatenate([R[b * 4 + j]["y"] for j in range(4)], axis=0) for b in range(2)], axis=0)
    y_sample = np.concatenate([R[c]["ys"] for c in range(NCORE)], axis=0).reshape(128, 1, D)
    nkp = np.stack([R[3]["nk"], R[7]["nk"]], axis=0).reshape(1, 2, 128, 4, 64)
    nvp = np.stack([R[3]["nv"], R[7]["nv"]], axis=0).reshape(1, 2, 128, 4, 64)
    npp = np.stack([R[3]["npool"][1:16], R[7]["npool"][1:16]], axis=0).reshape(1, 2, 15, 512)
    nks = np.concatenate([R[c]["nks"] for c in range(NCORE)], axis=0).reshape(1, 128, 128, 4, 64)
    nvs = np.concatenate([R[c]["nvs"] for c in range(NCORE)], axis=0).reshape(1, 128, 128, 4, 64)
    nps = np.concatenate([R[c]["nps"] for c in range(NCORE)], axis=0).reshape(1, 128, 15, 512)
    f = lambda a: np.ascontiguousarray(a, dtype=np.float32)
    return (f(y_prompt), f(y_sample), f(nkp), f(nvp), f(npp), f(nks), f(nvs), f(nps))
```

```python
import numpy as np
import ml_dtypes
from contextlib import ExitStack
import concourse.bass as bass
import concourse.mybir as mybir
from concourse.bass_utils import run_bass_kernel_spmd

F32 = mybir.dt.float32
BF16 = mybir.dt.bfloat16
AF = mybir.ActivationFunctionType
ALU = mybir.AluOpType

NCORE = 8
D = 1024
TOKC = 2048
NT = 4
NSMP = 16
FF = 2816
NHC = 22
EPS = 1e-6
RING = 4
ENGS = ("pe", "act", "dve", "pool", "sp")
DO_SAMPLE = True
STOP = None
EVAC_DVE = False


class Buf:
    __slots__ = ("name", "w", "rs")

    def __init__(self, name):
        self.name = name
        self.w = None
        self.rs = []


class DSem:
    def __init__(self, h):
        self.h = h
        self.count = 0


class Op:
    __slots__ = ("eng", "fn", "deps", "sig", "sem", "val", "dma", "epoch", "ndma")


class Sched:
    def __init__(self):
        self.ops = {e: [] for e in ENGS}
        self.epoch = 0
        self.all = []

    def add(self, eng, fn, r=(), w=(), dsem=None, ndma=1):
        op = Op()
        op.eng, op.fn, op.epoch = eng, fn, self.epoch
        op.dma = dsem is not None
        op.ndma = ndma
        op.sig = op.dma
        op.sem = dsem
        op.val = None
        if op.dma:
            dsem.count += 16 * ndma
            op.val = dsem.count
        deps = []
        for b in r:
            if b.w is not None:
                deps.append(b.w)
        for b in w:
            if b.w is not None:
                deps.append(b.w)
            deps.extend(b.rs)
        for b in r:
            b.rs.append(op)
        for b in w:
            b.w = op
            b.rs = []
        seen = set()
        op.deps = []
        for d in deps:
            if d is op or id(d) in seen:
                continue
            seen.add(id(d))
            if d.eng == "pe" and eng == "pe" and not d.dma and not op.dma:
                continue
            op.deps.append(d)
            d.sig = True
        self.ops[eng].append(op)
        return op

    def finalize(self, sems):
        for eng in ENGS:
            cnt = {}
            for op in self.ops[eng]:
                if op.dma or not op.sig:
                    continue
                c = cnt.get(op.epoch, 0) + 1
                cnt[op.epoch] = c
                op.sem = sems[eng][op.epoch]
                op.val = c

    def emit(self, eng, e):
        known = {}
        for op in self.ops[eng]:
            need = {}
            for d in op.deps:
                h = d.sem.h if d.dma else d.sem
                key = id(h)
                if key not in need or need[key][1] < d.val:
                    need[key] = (h, d.val)
            for key, (h, val) in need.items():
                if known.get(key, 0) >= val:
                    continue
                e.wait_ge(h, val)
                known[key] = val
            ins = op.fn(e)
            if op.sig:
                if op.dma:
                    lst = ins if isinstance(ins, (list, tuple)) else [ins]
                    assert len(lst) == op.ndma
                    for i_ in lst:
                        i_.then_inc(op.sem.h, 16)
                else:
                    ins.then_inc(op.sem, 1)


def build_program():
    nc = bass.Bass("TRN2", target_bir_lowering=False)

    def din(name, shape, dt=F32):
        return nc.dram_tensor(name, list(shape), dt, kind="ExternalInput").ap()

    def dout(name, shape, dt=F32):
        return nc.dram_tensor(name, list(shape), dt, kind="ExternalOutput").ap()

    xh = din("xh", [TOKC + 128, D])
    ph = din("ph", [TOKC, 256])
    xs = din("xs", [NSMP, D])
    ps = din("ps", [NSMP, 256])
    ck = din("ck", [NSMP, 128, 256])
    cv = din("cv", [NSMP, 128, 256])
    spool = din("spool", [NSMP, 15, 512])
    ln1 = din("ln1", [D])
    w_in = din("w_in", [D, 4096])
    pgw = din("pool_group_w", [4, 128, 128])
    pscale_d = din("pool_scale", [512])
    sinks_d = din("attn_sinks", [16])
    w_pb = din("w_pool_branch", [512, D])
    w_ab = din("w_attn_branch", [D, D])
    w_out = din("w_out", [D, D])
    ln2 = din("ln2", [D])
    w_fi = din("w_ffn_in", [D, 2 * FF])
    w_fo = din("w_ffn_out", [FF, D])
    w_pp = din("w_ple_proj", [256, D])
    plen = din("ple_norm", [D])
    w_pg = din("w_ple_gate", [D, D])
    fnorm = din("final_norm", [D])
    cs_d = din("cs", [128, 2 * 18 * 8])
    mask_d = din("maskd", [128, 2, 1024], BF16)
    rc16_d = din("rc16", [128, 64])
    ident_d = din("ident", [128, 128], BF16)
    identf_d = din("identf", [128, 128], F32)
    sel_d = din("sel", [128, 128], F32)

    y = dout("y", [TOKC, D])
    ys = dout("ys", [NSMP, D])
    nk = dout("nk", [128, 256])
    nv = dout("nv", [128, 256])
    npool = dout("npool", [16, 512])
    nks = dout("nks", [NSMP, 128, 256])
    nvs = dout("nvs", [NSMP, 128, 256])
    nps = dout("nps", [NSMP, 15, 512])

    NU_ = 33
    wscr = nc.dram_tensor("wscr", [NU_, 128, 4096], BF16, kind="Internal").ap()
    S = Sched()
    es = ExitStack()
    with es:
        def sb(name, shape, dt):
            return es.enter_context(nc.sbuf_tensor(name, list(shape), dt))

        x_tm = sb("x_tm", [128, 8, D], F32)
        x_aux = sb("x_aux", [128, D], F32)
        tmst = sb("tmst", [128, 4, D], BF16)
        xnT = sb("xnT", [128, 8, 640], BF16)
        u_ext = sb("u_ext", [128, 4, 528], F32)
        ptmp = sb("ptmp", [128, 2, 528], F32)
        m_sb = sb("m_sb", [128, 4, 528], BF16)
        mixT = sb("mixT", [128, 4, 528], BF16)
        qT = sb("qT", [128, 8, 528], BF16)
        kT = sb("kT", [128, 4, 640], BF16)
        vaug = sb("vaug", [128, 5, 4, 65], BF16)
        sg = sb("sg", [128, 16, 528], BF16)
        PT = sb("PT", [128, 2, 1024], BF16)
        masks = sb("masks", [128, 2, 1024], BF16)
        mrgT = sb("mrgT", [128, 8, 528], BF16)
        tmpf = sb("tmpf", [128, 2, 528], F32)
        p_tm = sb("p_tm", [128, 5, 256], F32)
        pT = sb("pT", [128, 2, 640], BF16)
        ework = sb("ework", [128, 2, D], F32)
        gate_tm = sb("gate_tm", [128, D], BF16)
        gains = sb("gains", [128, 4, D], F32)
        cs_sb = sb("cs_sb", [128, 2, 18, 8], F32)
        ring = sb("ring", [128, RING, 4096], BF16)
        scal = sb("scal", [128, 96], F32)
        ident = sb("ident_sb", [128, 128], BF16)
        identf = sb("identf_sb", [128, 128], F32)
        rc16 = sb("rc16_sb", [128, 4, 16], F32)
        pscale = sb("pscale", [128, 4], F32)
        es_sb = sb("es_sb", [128, 16], F32)
        Gw = sb("Gw", [128, 4, 128], BF16)
        kf = sb("kf", [128, 2, 512], F32)
        k_tm = sb("k_tm", [128, 2, 512], BF16)
        ropet = sb("ropet", [128, 4, 128], F32)
        ropeq = sb("ropeq", [128, 16, 16], F32)
        dn = sb("dn", [128, 2, 8], F32)
        Osb = sb("Osb", [128, 2, 256], F32)
        dummy = sb("dummy_sb", [128, 2], F32)
        nhalf = sb("nhalf", [128, 1], F32)
        ones_bf = sb("ones_bf", [128, 128], BF16)
        pp = [es.enter_context(nc.psum_tensor(f"pp{i}", [128, 1024], F32)) for i in range(4)]

        attn_tm = ework[:].rearrange("p a d -> p (a d)").bitcast(BF16).rearrange("p (a d) -> p a d", d=D)

        xflat = x_tm[:, 0:4, :].rearrange("p a d -> p (a d)")
        def bfv(ap_):
            return ap_.bitcast(BF16)
        cks_g = bfv(xflat[:, 0:512]).rearrange("p (s c) -> p s c", c=256)
        cvs_g = bfv(xflat[:, 512:1024]).rearrange("p (s c) -> p s c", c=256)
        ksT_g = bfv(xflat[:, 1024:1536]).rearrange("p (c k) -> p c k", k=128)
        st2 = xflat[:, 1536:2560].rearrange("p (h c) -> p h c", c=512)
        cks_B = bfv(xflat[:, 1536:2048]).rearrange("p (s c) -> p s c", c=256)
        cvs_B = bfv(xflat[:, 2048:2560]).rearrange("p (s c) -> p s c", c=256)
        us_tm = xflat[:16, 2560:3072]
        ssum = xflat[:16, 3072:3584]
        ms_bf = bfv(xflat[:16, 3584:3840])
        knew = bfv(xflat[:16, 3840:4096])
        gflat = gate_tm[:, :].bitcast(F32)
        PTs = bfv(gflat[:, 0:128])
        rds = gflat[:, 128:384]
        OTsb = gflat[:, 384:512]
        pflat = pT[:].rearrange("p a d -> p (a d)").bitcast(F32)
        qsh = bfv(pflat[:, 0:56]).rearrange("p (c t) -> p c t", t=16)
        qsel = bfv(pflat[:, 64:192]).rearrange("p (s h) -> p s h", h=16)
        sel = pflat[:, 192:320]

        def bank(b):
            return pp[b // 2][:, (b % 2) * 512:(b % 2) * 512 + 512]

        def bank_bf(b):
            return bank(b).bitcast(BF16)

        def pair(b):
            return pp[b // 2][:, :]

        BK = [Buf(f"bank{i}") for i in range(8)]

        class Banks:
            i = 0
            reserved = ()

            def one(self):
                while True:
                    b = self.i % 8
                    self.i += 1
                    if b not in self.reserved:
                        return b

            def pair(self):
                while True:
                    if self.i % 2:
                        self.i += 1
                    b = self.i % 8
                    self.i += 2
                    if b not in self.reserved and (b + 1) not in self.reserved:
                        return b
        banks = Banks()

        B_x = [Buf(f"x{i}") for i in range(8)]
        B_xaux = Buf("xaux")
        B_tm = [Buf(f"tm{i}") for i in range(4)]
        B_xnT = [Buf(f"xnT{i}") for i in range(5)]
        B_u = Buf("u")
        B_pt = [Buf("ptA"), Buf("ptB")]
        B_m = [Buf(f"m{g}") for g in range(4)]
        B_mix = [Buf(f"mix{g}") for g in range(4)]
        B_qT = [Buf(f"qT{i}") for i in range(5)]
        B_kT = [Buf(f"kT{i}") for i in range(5)]
        B_v = [Buf(f"v{i}") for i in range(5)]
        B_sg = [Buf(f"sg{i}") for i in range(16)]
        B_PT = [Buf("PT0"), Buf("PT1")]
        B_mrg = [Buf(f"mrg{i}") for i in range(8)]
        B_tmpf = [Buf("tmpf0"), Buf("tmpf1")]
        B_p = [Buf(f"p{i}") for i in range(5)]
        B_pT = [Buf(f"pT{i}") for i in range(5)]
        B_ew = [Buf("ew0"), Buf("ew1")]
        B_gate = Buf("gate")
        B_const = Buf("const")
        B_ring = [Buf(f"ring{i}") for i in range(RING)]
        B_kf = [Buf("kf0"), Buf("kf1")]
        B_ktm = [Buf("ktm0"), Buf("ktm1")]
        B_rope = Buf("rope")
        B_ropeq = Buf("ropeq")
        B_dn = [Buf("dn0"), Buf("dn1")]
        B_Osb = [Buf("Osb0"), Buf("Osb1")]
        B_out = Buf("out")
        B_scal = [Buf(f"scal{i}") for i in range(32)]
        B_smp = Buf("smp")
        SB = {k: Buf("s_" + k) for k in ("cks", "cvs", "cksB", "cvsB", "ksT", "st2", "us", "ssum", "ms", "knew", "PTs", "rds", "OTs", "qsh", "qsel", "sel")}

        nsem_epochs = NT + 1
        sems = {e: [es.enter_context(nc.semaphore(f"s_{e}{i}")) for i in range(nsem_epochs)] for e in ENGS}

        def dsem(name):
            return DSem(es.enter_context(nc.semaphore(name)))
        ds_ring = [dsem(f"d_ring{i}") for i in range(RING)]
        ds_ringh = [dsem(f"d_ringh{i}") for i in range(RING)]
        ds_wb = [dsem(f"d_wb{i}") for i in range(RING)]
        B_scr = [Buf(f"scr{i}") for i in range(NU_)]
        ds_x = [dsem(f"d_x{i}") for i in range(8)]
        ds_xaux = dsem("d_xaux")
        ds_p = [dsem(f"d_p{i}") for i in range(5)]
        ds_const = dsem("d_const")
        ds_yo = [dsem(f"d_yo{i}") for i in range(9)]
        ds_misc = dsem("d_misc")
        ds_gw = dsem("d_gw")
        ds_npool = dsem("d_npool")
        ds_smp = dsem("d_smp")
        ds_ck = [dsem("d_ck"), dsem("d_ckB")]
        ds_row = [dsem("d_row"), dsem("d_rowB")]
        ds_st = dsem("d_st")
        ds_nps = dsem("d_nps")
        ds_nkvs = dsem("d_nkvs")

        A = S.add
        scal_i = [0]

        def nscal():
            i = scal_i[0] % 32
            scal_i[0] += 1
            return i

        B_ca = Buf("constA")
        ds_ca = dsem("d_ca")

        def ld_const_a(e):
            L = []
            L.append(e.dma_start(out=ident[:], in_=ident_d))
            L.append(e.dma_start(out=gains[:, 0, :], in_=ln1.partition_broadcast(128)))
            return L

        def ld_const(e):
            L = []
            L.append(e.dma_start(out=identf[:], in_=identf_d))
            L.append(e.dma_start(out=masks[:], in_=mask_d))
            L.append(e.dma_start(out=rc16[:].rearrange("p g c -> p (g c)"), in_=rc16_d))
            L.append(e.dma_start(out=gains[:, 1, :], in_=ln2.partition_broadcast(128)))
            L.append(e.dma_start(out=gains[:, 2, :], in_=plen.partition_broadcast(128)))
            L.append(e.dma_start(out=gains[:, 3, :], in_=fnorm.partition_broadcast(128)))
            L.append(e.dma_start(out=cs_sb[:].rearrange("p a b c -> p (a b c)"), in_=cs_d))
            for g in range(4):
                L.append(e.dma_start(out=pscale[:, g:g + 1], in_=pscale_d[g * 128:(g + 1) * 128].rearrange("(p o) -> p o", o=1)))
            L.append(e.dma_start(out=es_sb[:], in_=sinks_d.partition_broadcast(128)))
            return L
        A("sp", ld_const_a, w=[B_ca], dsem=ds_ca, ndma=2)
        A("pool", lambda e: [e.dma_start(out=Gw[:], in_=pgw.rearrange("g c d -> c g d"))], w=[B_const], dsem=ds_gw, ndma=1)
        A("pool", lambda e: e.memset(vaug[:, :, :, 64:65], 1.0), w=B_v)
        A("pool", lambda e: e.memset(ones_bf[:], 1.0), w=[B_const])
        A("pool", lambda e: e.memset(nhalf[:], -0.5), w=[B_ca])

        unit_ctr = [0]

        def wload(srcs):
            slot = unit_ctr[0] % RING
            unit_ctr[0] += 1

            def fn(e, srcs=srcs, slot=slot):
                L = []
                for (src, coff, ncols, nkc, width) in srcs:
                    dst = ring[:, slot, 0:nkc * width].rearrange("p (k c) -> p k c", c=width)[:, :, coff:coff + ncols]
                    L.append(e.dma_start(out=dst, in_=src.rearrange("(k p) c -> p k c", p=128)))
                return L
            A("pool", fn, r=(B_x[0:4] if 0 < unit_ctr[0] - 1 < RING else []), w=[B_ring[slot]], dsem=ds_ring[slot], ndma=len(srcs))
            return slot

        def rview(slot, nkc, width):
            return ring[:, slot, 0:nkc * width].rearrange("p (k c) -> p k c", c=width)

        def unit_list():
            U = []
            for c0 in (0, 2048, 2560, 3072, 3584, 512, 1024, 1536):
                U.append(("win", c0))
            U.append(("wpb",))
            U.append(("wab", 0)); U.append(("wab", 512))
            U.append(("wout", 0)); U.append(("wout", 512))
            for n in range(11):
                U.append(("ffi", n))
            for half in range(2):
                for k0 in (0, 8, 16):
                    U.append(("ffo", half, k0))
            U.append(("wpg", 0)); U.append(("wpg", 512))
            U.append(("wpp",))
            return U
        UL = unit_list()
        NU = len(UL)
        issued = [0]
        slot_of = {}

        def issue_unit(gidx):
            un = gidx % NU
            if gidx >= NU:
                slot = unit_ctr[0] % RING
                unit_ctr[0] += 1
                A("sp", lambda e, slot=slot, un=un: [e.dma_start(out=ring[:, slot, :], in_=wscr[un])],
                  r=[B_scr[un]], w=[B_ring[slot]], dsem=ds_ringh[slot])
                slot_of[gidx] = slot
                return
            u = UL[gidx % NU]
            if u[0] == "win":
                s = wload([(w_in[:, u[1]:u[1] + 512], 0, 512, 8, 512)])
            elif u[0] == "wpb":
                s = wload([(w_pb[:, :], 0, 1024, 4, 1024)])
            elif u[0] == "wab":
                s = wload([(w_ab[:, u[1]:u[1] + 512], 0, 512, 8, 512)])
            elif u[0] == "wout":
                s = wload([(w_out[:, u[1]:u[1] + 512], 0, 512, 8, 512)])
            elif u[0] == "ffi":
                n = u[1]
                s = wload([(w_fi[:, n * 256:n * 256 + 256], 0, 256, 8, 512),
                           (w_fi[:, FF + n * 256:FF + n * 256 + 256], 256, 256, 8, 512)])
            elif u[0] == "ffo":
                half, k0 = u[1], u[2]
                nkc = min(8, NHC - k0)
                s = wload([(w_fo[k0 * 128:(k0 + nkc) * 128, half * 512:half * 512 + 512], 0, 512, nkc, 512)])
            elif u[0] == "wpg":
                s = wload([(w_pg[:, u[1]:u[1] + 512], 0, 512, 8, 512)])
            elif u[0] == "wpp":
                s = wload([(w_pp[:, :], 0, 1024, 2, 1024)])
            slot_of[gidx] = s
            if NT > 1:
                A("sp", lambda e, s=s, un=un: [e.dma_start(out=wscr[un], in_=ring[:, s, :])], r=[B_ring[s]], w=[B_scr[un]], dsem=ds_wb[s])

        assert NU == NU_
        total_units = NU * NT

        released = [0]

        def fill():
            while issued[0] < min(total_units, released[0] + RING):
                issue_unit(issued[0])
                issued[0] += 1

        def release(k):
            released[0] += k
            fill()

        def use_unit(gidx):
            fill()
            assert gidx < issued[0], (gidx, issued[0], released[0])
            return slot_of[gidx]

        def rmsnorm_to_bf16(src_ap, srcbuf, nt, gidx, dst_ap, dstbuf):
            si = nscal()
            c = si * 3
            A("act", lambda e: e.activation(out=dst_ap, in_=src_ap, func=AF.Square, accum_out=scal[:nt, c:c + 1]),
              r=[srcbuf], w=[dstbuf, B_scal[si]])
            A("dve", lambda e: e.tensor_scalar(out=scal[:nt, c + 1:c + 2], in0=scal[:nt, c:c + 1], scalar1=1.0 / D, scalar2=EPS, op0=ALU.mult, op1=ALU.add),
              r=[], w=[B_scal[si]])
            A("pool", lambda e: e.tensor_tensor(out=scal[:nt, c + 2:c + 3], in0=scal[:nt, c + 1:c + 2], in1=nhalf[:nt, :], op=ALU.pow), r=[B_ca], w=[B_scal[si]])
            A("dve", lambda e: e.scalar_tensor_tensor(out=dst_ap, in0=src_ap, scalar=scal[:nt, c + 2:c + 3], in1=gains[:nt, gidx, :],
                                                       op0=ALU.mult, op1=ALU.mult),
              r=[srcbuf, B_scal[si], (B_ca if gidx == 0 else B_const)], w=[dstbuf])

        def transpose_to(src_fn, srcbuf, nt, nch, dst_ap, dstbufs, evac="act"):
            if STOP == 'x1':
                return
            b = banks.one()
            pv = bank_bf(b)

            def fn(e):
                ins = None
                for c in range(nch):
                    ins = e.transpose(out=pv[:, c * 128:c * 128 + nt], in_=src_fn(c), identity=ident[:nt, :nt])
                return ins
            A("pe", fn, r=(list(srcbuf) if isinstance(srcbuf, (list, tuple)) else [srcbuf]) + [B_ca], w=[BK[b]])
            src = pv[:, 0:nch * 128].rearrange("p (c t) -> p c t", t=128)[:, :, 0:nt]
            if STOP == 'x2':
                return
            ev_i[0] += 1
            if EVAC_DVE and nt == 128 and ev_i[0] % 2 == 0:
                evac = "dve"
            if evac == "dve":
                A("dve", lambda e: e.tensor_copy(out=dst_ap, in_=src), r=[BK[b]], w=dstbufs)
            else:
                A("act", lambda e: e.activation(out=dst_ap, in_=src, func=AF.Copy), r=[BK[b]], w=dstbufs)

        def transpose_f32_to(src_fn, srcbufs, nt, nch, dst_ap, dstbufs):
            if nch > 4:
                b = banks.pair()
                pv = pair(b)
                bks = [BK[b], BK[b + 1]]
            else:
                b = banks.one()
                pv = bank(b)
                bks = [BK[b]]

            def fn(e):
                ins = None
                for c in range(nch):
                    ins = e.transpose(out=pv[:, c * 128:c * 128 + nt], in_=src_fn(c), identity=identf[:nt, :nt])
                return ins
            A("pe", fn, r=list(srcbufs) + [B_const], w=bks)

            def ev(e):
                ins = None
                for c0 in range(0, nch, 4):
                    c1 = min(nch, c0 + 4)
                    ins = e.activation(out=dst_ap[:, c0:c1, :], in_=pv[:, c0 * 128:c1 * 128].rearrange("p (c t) -> p c t", t=128)[:, :, 0:nt], func=AF.Copy)
                return ins
            A("act", ev, r=bks, w=dstbufs)

        tm_i = [0]
        kf_i = [0]
        ev_i = [0]
        pre_tis = {}
        done_TX = {}
        done_aux = {}
        tail_ops = []
        ckb = []

        def ld_cast(g4):
            ckt, cvt, bk_, bv_ = ckb[g4 % 2]
            extra = [SB["st2"]] if g4 % 2 else []
            A("pool", lambda e, g4=g4, ckt=ckt, cvt=cvt: [e.dma_start(out=ckt, in_=ck[g4 * 4:(g4 + 1) * 4].rearrange("s k c -> k s c")),
                                                          e.dma_start(out=cvt, in_=cv[g4 * 4:(g4 + 1) * 4].rearrange("s k c -> k s c"))],
              r=[B_smp], w=[bk_, bv_] + extra, dsem=ds_ck[g4 % 2], ndma=2)

        def ld_row(g4):
            ckt, cvt, bk_, bv_ = ckb[g4 % 2]
            A("sp", lambda e, g4=g4, ckt=ckt, cvt=cvt: [e.dma_start(out=ckt[0:1, :, :], in_=knew[g4 * 4:(g4 + 1) * 4, 0:256]),
                                                        e.dma_start(out=cvt[0:1, :, :], in_=knew[g4 * 4:(g4 + 1) * 4, 256:512])],
              r=[SB["knew"]], w=[bk_, bv_], dsem=ds_row[g4 % 2], ndma=2)

        def ntm():
            i = tm_i[0] % 4
            tm_i[0] += 1
            return i

        gunit = [0]

        def next_unit():
            g = gunit[0]
            gunit[0] += 1
            return use_unit(g)

        def load_x_tile(t):
            slot = t % 2
            for b in range(4):
                r0 = 128 + t * 512 + b * 128
                A("sp", lambda e, r0=r0, i=slot * 4 + b: [e.dma_start(out=x_tm[:, i, :], in_=xh[r0:r0 + 128, :])],
                  w=[B_x[slot * 4 + b]], dsem=ds_x[slot * 4 + b])

        load_x_tile(0)
        A("sp", ld_const, w=[B_const], dsem=ds_const, ndma=12)
        A("sp", lambda e: [e.dma_start(out=x_aux[:, :], in_=xh[0:128, :])], w=[B_xaux], dsem=ds_xaux)

        def do_tile(t):
            S.epoch = t
            slot = t % 2
            last = (t == NT - 1)
            has_s = last and DO_SAMPLE
            blocks = []
            for b in range(4):
                blocks.append(("p", 128, x_tm[:, slot * 4 + b, :], B_x[slot * 4 + b], b * 128, b, 1 + t * 4 + b))
            if t == 0:
                blocks.append(("h", 128, x_aux[:, :], B_xaux, 512, 4, 0))
            if has_s:
                blocks.append(("s", NSMP, x_aux[:NSMP, :], B_xaux, 512, 4, 17))
            for b in range(4):
                r0 = t * 512 + b * 128
                A("sp", lambda e, r0=r0, b=b: [e.dma_start(out=p_tm[:, b, :], in_=ph[r0:r0 + 128, :])],
                  w=[B_p[b]], dsem=ds_p[b])
            if DO_SAMPLE and t == max(NT - 2, 0) and NT > 1:
                A("sp", lambda e: [e.dma_start(out=x_aux[:NSMP, :], in_=xs)], w=[B_xaux], dsem=ds_xaux)
                A("sp", lambda e: [e.dma_start(out=p_tm[:NSMP, 4, :], in_=ps)], w=[B_p[4]], dsem=ds_p[4])
            ncol = 528 if has_s else 512
            if STOP == 'c0':
                return

            for gi_, grp in enumerate((blocks[0:4], blocks[4:])):
                if gi_ == 1 and done_aux.get(t):
                    continue
                if gi_ == 0 and t in pre_tis:
                    tis = pre_tis[t]
                else:
                    tis = []
                    for (kind, nt, xap, xbuf, col0, cb, trow) in grp:
                        ti = ntm()
                        tis.append(ti)
                        rmsnorm_to_bf16(xap, xbuf, nt, 0, tmst[:nt, ti, :], B_tm[ti])
                if gi_ == 0 and done_TX.get(t):
                    continue
                if t == 0 and gi_ == 0:
                    fill()
                for ti, (kind, nt, xap, xbuf, col0, cb, trow) in zip(tis, grp):
                    if kind == "h":
                        transpose_to(lambda c, ti=ti, nt=nt: tmst[:nt, ti, c * 128:(c + 1) * 128], B_tm[ti], nt, 8,
                                     xnT[:, :, col0:col0 + nt], [B_xnT[cb]])
                    else:
                        transpose_to(lambda c, ti=ti, nt=nt: tmst[:nt, ti, c * 128:(c + 1) * 128], B_tm[ti], nt, 8,
                                     qT[:, :, col0:col0 + nt], [B_qT[cb]])
            if STOP in ('x', 'x1', 'x2'):
                return
            if has_s or t == 0:
                while tail_ops:
                    tail_ops.pop(0)()
                if t + 1 < NT:
                    load_x_tile(t + 1)
            if has_s:
                while tail_ops:
                    tail_ops.pop(0)()
                A("pool", lambda e: e.memset(dummy[:, 0:1], 0.0), w=[B_smp] + list(SB.values()) + B_x[0:4] + [B_gate] + B_pT)
                A("sp", lambda e: [e.dma_start(out=sel, in_=sel_d),
                                   e.dma_start(out=st2[0:120, 0, :], in_=spool[0:8].rearrange("s r c -> (s r) c")),
                                   e.dma_start(out=st2[0:120, 1, :], in_=spool[8:16].rearrange("s r c -> (s r) c"))],
                  r=[B_smp], w=[SB["sel"], SB["st2"]], dsem=ds_st, ndma=3)
            xn_main = B_qT[0:4]
            xn_all = B_qT[0:4] + ([B_qT[4]] if has_s else [])
            hn_all = B_xnT[0:4] + ([B_xnT[4]] if has_s else [])

            s_u = next_unit()
            wv = rview(s_u, 8, 512)
            for g in range(4):
                b = banks.one()

                def fn(e, g=g, b=b, wv=wv):
                    ins = None
                    for kc in range(8):
                        ins = e.matmul(bank(b), lhsT=wv[:, kc, g * 128:(g + 1) * 128], rhs=qT[:, kc, 0:512],
                                       start=(kc == 0), stop=(kc == 7))
                    return ins
                A("pe", fn, r=xn_main + [B_ring[s_u]], w=[BK[b]])
                A("act", lambda e, g=g, b=b: e.activation(out=u_ext[:, g, 16:528], in_=bank(b), func=AF.Copy),
                  r=[BK[b]], w=[B_u])
            if t == 0:
                b = banks.one()

                def fn(e, b=b, wv=wv):
                    ins = None
                    for g in range(4):
                        for kc in range(8):
                            ins = e.matmul(bank(b)[:, g * 16:(g + 1) * 16], lhsT=wv[:, kc, g * 128:(g + 1) * 128],
                                           rhs=xnT[:, kc, 624:640], start=(kc == 0), stop=(kc == 7))
                    return ins
                A("pe", fn, r=[B_xnT[4], B_ring[s_u]], w=[BK[b]])
                A("act", lambda e, b=b: e.activation(out=u_ext[:, :, 0:16], in_=bank(b)[:, 0:64].rearrange("p (g c) -> p g c", c=16),
                                                   func=AF.Copy), r=[BK[b]], w=[B_u])
            if has_s:
                b = banks.one()

                def fn(e, b=b, wv=wv):
                    ins = None
                    for kc in range(8):
                        ins = e.matmul(bank(b)[:NSMP, :], lhsT=qT[:, kc, 512:512 + NSMP], rhs=wv[:, kc, :],
                                       start=(kc == 0), stop=(kc == 7))
                    return ins
                A("pe", fn, r=[B_qT[4], B_ring[s_u]], w=[BK[b]])
                A("act", lambda e, b=b: e.activation(out=us_tm, in_=bank(b)[:NSMP, :], func=AF.Copy), r=[BK[b], B_smp], w=[SB["us"]])

            if STOP == 'u':
                return
            if STOP == 'w5':
                for sl in range(1, 4):
                    bq = banks.one()
                    A("pe", lambda e, sl=sl, bq=bq: e.matmul(bank(bq)[:, 0:512], lhsT=xnT[:, 0, 0:128], rhs=ring[:, sl, 0:512], start=True, stop=True),
                      r=[B_ring[sl], B_xnT[0]], w=[BK[bq]])
                return
            if STOP == 'w6':
                for sl in range(1, 4):
                    bq = banks.one()
                    def f6(e, sl=sl, bq=bq):
                        ins = None
                        for kc in range(8):
                            ins = e.matmul(bank(bq)[:, 0:512], lhsT=xnT[:, kc, 0:128], rhs=ring[:, sl, kc * 512:(kc + 1) * 512], start=(kc == 0), stop=(kc == 7))
                        return ins
                    A("pe", f6, r=[B_ring[sl], B_xnT[0]], w=[BK[bq]])
                return
            if STOP in ('w2', 'w4'):
                for sl in range(1, 2 if STOP == 'w2' else 4):
                    bq = banks.one()
                    A("pe", lambda e, sl=sl, bq=bq: e.matmul(bank(bq)[:, 0:16], lhsT=ring[:, sl, 0:128], rhs=xnT[:, 0, 0:16], start=True, stop=True),
                      r=[B_ring[sl], B_xnT[0]], w=[BK[bq]])
                return
            release(1)
            E = u_ext
            for g in range(4):
                w_ = 2 << g
                cur = None
                A("pool", lambda e, g=g: e.tensor_tensor(out=ptmp[:, 0, 1:528], in0=E[:, g, 1:528], in1=E[:, g, 0:527], op=ALU.add),
                  r=[B_u], w=[B_pt[0]])
                ci = 0
                sh = 2
                lo = 1
                while sh < w_:
                    lo2 = lo + sh
                    A("pool", lambda e, ci=ci, sh=sh, lo2=lo2: e.tensor_tensor(out=ptmp[:, 1 - ci, lo2:528], in0=ptmp[:, ci, lo2:528],
                                                                               in1=ptmp[:, ci, lo2 - sh:528 - sh], op=ALU.add),
                      r=[B_pt[ci]], w=[B_pt[1 - ci]])
                    ci = 1 - ci
                    lo = lo2
                    sh *= 2
                A("dve", lambda e, g=g, ci=ci, w_=w_: e.scalar_tensor_tensor(out=m_sb[:, g, 0:512], in0=ptmp[:, ci, 16:528], scalar=1.0 / w_,
                                                                              in1=E[:, g, 16:528], op0=ALU.mult, op1=ALU.subtract),
                  r=[B_pt[ci], B_u], w=[B_m[g]])
                if t == 0:
                    A("dve", lambda e, g=g, ci=ci: e.tensor_tensor(out=ropet[:, 0, 0:16], in0=ptmp[:, ci, 16:32], in1=rc16[:, g, :], op=ALU.mult),
                      r=[B_pt[ci], B_const], w=[B_rope])
                    A("dve", lambda e, g=g: e.tensor_tensor(out=m_sb[:, g, 0:16], in0=ropet[:, 0, 0:16], in1=E[:, g, 16:32], op=ALU.subtract),
                      r=[B_rope, B_u], w=[B_m[g]])
            if last:
                b = banks.one()

                def fn(e, b=b):
                    ins = None
                    for g in range(4):
                        ins = e.transpose(out=bank(b)[:16, g * 128:(g + 1) * 128], in_=u_ext[:, g, 512:528], identity=identf[:, :])
                    return ins
                A("pe", fn, r=[B_u, B_const], w=[BK[b]])
                A("act", lambda e, b=b: e.activation(out=ework[:16, 1, 0:512], in_=bank(b)[:16, :], func=AF.Copy), r=[BK[b]], w=[B_ew[1]])
                A("sp", lambda e: [e.dma_start(out=npool, in_=ework[:16, 1, 0:512])], r=[B_ew[1], B_out], dsem=ds_npool)
            if not last:
                A("pool", lambda e: e.tensor_copy(out=u_ext[:, :, 0:16], in_=u_ext[:, :, 512:528]), r=[], w=[B_u])

            if has_s:
                bS = banks.one()

                def fnS(e, bS=bS):
                    ins = None
                    selv = sel.rearrange("p (h g c) -> p h g c", h=2, g=4)
                    for g in range(4):
                        for hf in range(2):
                            ins = e.matmul(bank(bS)[:16, g * 128:(g + 1) * 128], lhsT=selv[0:120, hf, g, :], rhs=st2[0:120, hf, g * 128:(g + 1) * 128],
                                           start=(hf == 0), stop=(hf == 1))
                    return ins
                A("pe", fnS, r=[SB["sel"], SB["st2"], B_smp], w=[BK[bS]])
                A("dve", lambda e, bS=bS: e.tensor_tensor(out=ssum, in0=bank(bS)[:16, :], in1=us_tm, op=ALU.add), r=[BK[bS], SB["us"], B_smp], w=[SB["ssum"]])
                for g in range(4):
                    w_ = 2 << g
                    A("dve", lambda e, g=g, w_=w_: e.scalar_tensor_tensor(out=ms_bf[:, g * 128:(g + 1) * 128], in0=ssum[:, g * 128:(g + 1) * 128],
                                                                         scalar=1.0 / w_, in1=us_tm[:, g * 128:(g + 1) * 128],
                                                                         op0=ALU.mult, op1=ALU.subtract), r=[SB["ssum"], SB["us"]], w=[SB["ms"]])
                transpose_to(lambda c: ms_bf[:, c * 128:(c + 1) * 128], SB["ms"], NSMP, 4, m_sb[:, :, 512:528], B_m)
                ckb.extend([(cks_g, cvs_g, SB["cks"], SB["cvs"]), (cks_B, cvs_B, SB["cksB"], SB["cvsB"])])
                ld_cast(0)
                ld_cast(1)
                A("sp", lambda e: [e.dma_start(out=nps[:, 14, :], in_=us_tm),
                                   e.dma_start(out=nps[:, 0:14, :], in_=spool[:, 1:15, :])], r=[SB["us"], B_out], dsem=ds_nps, ndma=2)

            for gi in range(4):
                s_g = next_unit()
                wv_ = rview(s_g, 8, 512)
                for mc in range(4):
                    j = gi * 4 + mc
                    b = banks.pair()

                    def fn(e, b=b, mc=mc, wv_=wv_):
                        ins = None
                        for kc in range(8):
                            ins = e.matmul(pair(b)[:, 0:512], lhsT=wv_[:, kc, mc * 128:(mc + 1) * 128], rhs=qT[:, kc, 0:512],
                                           start=(kc == 0), stop=(kc == 7))
                        if has_s:
                            for kc in range(8):
                                ins = e.matmul(pair(b)[:, 512:528], lhsT=wv_[:, kc, mc * 128:(mc + 1) * 128], rhs=qT[:, kc, 512:528],
                                               start=(kc == 0), stop=(kc == 7))
                        return ins
                    A("pe", fn, r=xn_all + [B_ring[s_g]], w=[BK[b], BK[b + 1]])
                    A("act", lambda e, b=b, j=j: e.activation(out=sg[:, j, 0:ncol], in_=pair(b)[:, 0:ncol], func=AF.Sigmoid),
                      r=[BK[b], BK[b + 1]], w=[B_sg[j]])
                release(1)

            if STOP == 'mix':
                return
            if not (has_s or t == 0):
                while tail_ops:
                    tail_ops.pop(0)()
                if t + 1 < NT:
                    load_x_tile(t + 1)
            s_q0 = next_unit()
            s_q1 = next_unit()
            s_kv = next_unit()
            wq0, wq1, wkv = rview(s_q0, 8, 512), rview(s_q1, 8, 512), rview(s_kv, 8, 512)
            prev_defer = []
            for (kind, nt, xap, xbuf, col0, cb, trow) in blocks:
                defer = []
                cosb = cs_sb[:nt, 0, trow, :]
                sinb = cs_sb[:nt, 1, trow, :]
                if kind != "h":
                    b = banks.pair()

                    def fn(e, b=b, col0=col0, nt=nt):
                        ins = None
                        for hf, wv_ in ((0, wq0), (1, wq1)):
                            for kc in range(8):
                                ins = e.matmul(pair(b)[:nt, hf * 512:(hf + 1) * 512], lhsT=qT[:, kc, col0:col0 + nt], rhs=wv_[:, kc, :],
                                               start=(kc == 0), stop=(kc == 7))
                        return ins
                    A("pe", fn, r=[B_qT[cb], B_ring[s_q0], B_ring[s_q1]], w=[BK[b], BK[b + 1]])
                    if STOP == 'q0':
                        continue
                    if STOP in ('k0', 'k1', 'k2', 'k3'):
                        pass
                    ti = ntm()
                    def qcopy(e, b=b, ti=ti, nt=nt):
                        e.activation(out=tmst[:nt, ti, 0:512], in_=pair(b)[:nt, 0:512], func=AF.Copy)
                        return e.activation(out=tmst[:nt, ti, 512:1024], in_=pair(b)[:nt, 512:1024], func=AF.Copy)
                    A("act", qcopy, r=[BK[b], BK[b + 1]], w=[B_tm[ti]])
                    if STOP == 'q1':
                        continue
                    Pv = pair(b)[:nt, :].rearrange("p (h d) -> p h d", d=64)
                    Qv = tmst[:nt, ti, :].rearrange("p (h d) -> p h d", d=64)
                    cb_ = cosb.unsqueeze(1).broadcast_to([nt, 16, 8])
                    sb_ = sinb.unsqueeze(1).broadcast_to([nt, 16, 8])
                    rt = [ropet[:nt, k, :].rearrange("p (h d) -> p h d", d=8) for k in range(4)]

                    A("act", lambda e, Pv=Pv, nt=nt: e.activation(out=ropeq[:nt, :, :], in_=Pv[:, :, 0:16], func=AF.Copy),
                      r=[BK[b], BK[b + 1]], w=[B_ropeq])
                    Rq = ropeq[:nt, :, :]

                    def rope_fn(e, Rq=Rq, cb_=cb_, sb_=sb_, rt=rt):
                        e.tensor_tensor(out=rt[0], in0=Rq[:, :, 0:8], in1=cb_, op=ALU.mult)
                        e.tensor_tensor(out=rt[1], in0=Rq[:, :, 8:16], in1=sb_, op=ALU.mult)
                        e.tensor_tensor(out=rt[2], in0=Rq[:, :, 8:16], in1=cb_, op=ALU.mult)
                        return e.tensor_tensor(out=rt[3], in0=Rq[:, :, 0:8], in1=sb_, op=ALU.mult)
                    A("dve", rope_fn, r=[B_ropeq, B_const], w=[B_rope])

                    def rope_fn2(e, Qv=Qv, rt=rt):
                        e.tensor_tensor(out=Qv[:, :, 0:8], in0=rt[0], in1=rt[1], op=ALU.subtract)
                        return e.tensor_tensor(out=Qv[:, :, 8:16], in0=rt[2], in1=rt[3], op=ALU.add)
                    if STOP == 'q2a':
                        continue
                    A("dve", rope_fn2, r=[B_rope], w=[B_tm[ti]])
                    if STOP == 'q2':
                        continue
                    defer.append(lambda ti=ti, nt=nt, col0=col0, cb=cb: transpose_to(
                        lambda c, ti=ti, nt=nt: tmst[:nt, ti, c * 128:(c + 1) * 128], B_tm[ti], nt, 8, qT[:, :, col0:col0 + nt], [B_qT[cb]]))
                    if kind == "s":
                        defer.append(lambda ti=ti, nt=nt: transpose_to(
                            lambda c, ti=ti, nt=nt: tmst[:nt, ti, 64 + c * 128:64 + (c + 1) * 128], [B_tm[ti], B_smp], nt, 7, qsh[:, :, 0:nt], [SB["qsh"]]))
                if STOP in ('q3', 'q0', 'q1', 'q2', 'q2a'):
                    continue
                b = banks.one()

                XNt, XNb = (xnT, B_xnT) if kind == "h" else (qT, B_qT)

                def fnkv(e, b=b, col0=col0, nt=nt, XNt=XNt):
                    ins = None
                    for kc in range(8):
                        ins = e.matmul(bank(b)[:nt, :], lhsT=XNt[:, kc, col0:col0 + nt], rhs=wkv[:, kc, :], start=(kc == 0), stop=(kc == 7))
                    return ins
                A("pe", fnkv, r=[XNb[cb], B_ring[s_kv]], w=[BK[b]])
                kf_i[0] += 1
                kfi = kf_i[0] % 2
                A("dve", lambda e, b=b, kfi=kfi, nt=nt: e.tensor_copy(out=kf[:nt, kfi, :], in_=bank(b)[:nt, :]),
                  r=[BK[b]], w=[B_kf[kfi]])
                if STOP == 'k1':
                    continue
                Pk = bank(b)[:nt, 0:256].rearrange("p (h d) -> p h d", d=64)
                Kv = kf[:nt, kfi, 0:256].rearrange("p (h d) -> p h d", d=64)
                cb4 = cosb.unsqueeze(1).broadcast_to([nt, 4, 8])
                sb4 = sinb.unsqueeze(1).broadcast_to([nt, 4, 8])
                rtk = [ropet[:nt, k, 0:32].rearrange("p (h d) -> p h d", d=8) for k in range(4)]

                def ropek(e, Kv=Kv, cb4=cb4, sb4=sb4, rtk=rtk):
                    e.tensor_tensor(out=rtk[0], in0=Kv[:, :, 0:8], in1=cb4, op=ALU.mult)
                    e.tensor_tensor(out=rtk[1], in0=Kv[:, :, 8:16], in1=sb4, op=ALU.mult)
                    e.tensor_tensor(out=rtk[2], in0=Kv[:, :, 8:16], in1=cb4, op=ALU.mult)
                    return e.tensor_tensor(out=rtk[3], in0=Kv[:, :, 0:8], in1=sb4, op=ALU.mult)
                A("dve", ropek, r=[B_kf[kfi], B_const], w=[B_rope])

                def ropek2(e, Kv=Kv, rtk=rtk):
                    e.tensor_tensor(out=Kv[:, :, 0:8], in0=rtk[0], in1=rtk[1], op=ALU.subtract)
                    return e.tensor_tensor(out=Kv[:, :, 8:16], in0=rtk[2], in1=rtk[3], op=ALU.add)
                A("dve", ropek2, r=[B_rope], w=[B_kf[kfi]])
                if STOP == 'k2':
                    continue
                if kind == "s":
                    A("act", lambda e, kfi=kfi: e.activation(out=knew, in_=kf[:NSMP, kfi, :], func=AF.Copy), r=[B_kf[kfi], B_smp], w=[SB["knew"]])
                    A("sp", lambda e, kfi=kfi: [e.dma_start(out=nks[:, 127, :], in_=kf[:NSMP, kfi, 0:256]),
                                                e.dma_start(out=nvs[:, 127, :], in_=kf[:NSMP, kfi, 256:512])],
                      r=[B_kf[kfi], B_out], dsem=ds_nkvs, ndma=2)
                else:
                    slot_k = (col0 // 128 + 1) if kind == "p" else 0
                    kti = kfi
                    def kdup(e, kfi=kfi, kti=kti, nt=nt):
                        kd = k_tm[:nt, kti, :].rearrange("p (h a d) -> p h a d", a=2, d=64)
                        src = kf[:nt, kfi, 0:256].rearrange("p (h d) -> p h d", d=64)
                        e.tensor_copy(out=kd[:, :, 0, :], in_=src)
                        return e.tensor_copy(out=kd[:, :, 1, :], in_=src)
                    A("pool", kdup, r=[B_kf[kfi]], w=[B_ktm[kti]])
                    A("pool", lambda e, kfi=kfi, slot_k=slot_k, nt=nt: e.tensor_copy(out=vaug[:nt, slot_k, :, 0:64],
                                                                                   in_=kf[:nt, kfi, 256:512].rearrange("p (h d) -> p h d", d=64)),
                      r=[B_kf[kfi]], w=[B_v[slot_k]])
                    if STOP == 'k3':
                        continue
                    defer.append(lambda kti=kti, nt=nt, slot_k=slot_k: transpose_to(
                        lambda c, kti=kti, nt=nt: k_tm[:nt, kti, c * 128:(c + 1) * 128], B_ktm[kti], nt, 4,
                        kT[:, :, slot_k * 128:slot_k * 128 + nt], [B_kT[slot_k]]))
                    if last and kind == "p" and col0 == 384:
                        A("sp", lambda e, kfi=kfi: [e.dma_start(out=nk, in_=kf[:, kfi, 0:256]), e.dma_start(out=nv, in_=kf[:, kfi, 256:512])],
                          r=[B_kf[kfi], B_out], dsem=ds_misc, ndma=2)
                for f_ in prev_defer:
                    f_()
                prev_defer = defer

            for f_ in prev_defer:
                f_()
            if STOP in ('qkv', 'q0', 'q1', 'q2', 'q2a', 'q3', 'k1', 'k2', 'k3'):
                return
            release(3)
            if STOP == 'gates':
                return
            if t == 0:
                A("act", lambda e: e.activation(out=es_sb[:], in_=es_sb[:], func=AF.Exp), r=[], w=[B_const])
            mk = 0 if t == 0 else 1
            segs = [(0, 0, 0, 128), (1, 128, 0, 256), (2, 384, 128, 128), (2, 512, 256, 128), (3, 640, 256, 256), (4, 896, 384, 128)]

            def emit_scores(h):
                cq, po, kv = h // 2, (h % 2) * 64, h // 4
                b = banks.pair()

                def fn(e, b=b, po=po, kv=kv, cq=cq):
                    ins = None
                    for hb in range(2):
                        ins = e.matmul(pair(b)[:, hb * 512:(hb + 1) * 512], lhsT=ident[:, :], rhs=masks[:, mk, hb * 512:(hb + 1) * 512], start=True, stop=False)
                    for si_, (s_, c0, q0, n_) in enumerate(segs):
                        ins = e.matmul(pair(b)[:, c0:c0 + n_], lhsT=kT[po:po + 64, kv, s_ * 128:(s_ + 1) * 128], rhs=qT[po:po + 64, cq, q0:q0 + n_],
                                       start=False, stop=(si_ in (2, 5)))
                    return ins
                A("pe", fn, r=B_qT[0:4] + B_kT + [B_const], w=[BK[b], BK[b + 1]])
                pi = h % 2
                A("act", lambda e, b=b, pi=pi: e.activation(out=PT[:, pi, :], in_=pair(b)[:, :], func=AF.Exp, scale=0.125),
                  r=[BK[b], BK[b + 1]], w=[B_PT[pi]])

            def emit_pv(h):
                kv = h // 4
                pi = h % 2
                ob = banks.one()

                def fnpv(e, ob=ob, pi=pi, kv=kv):
                    ins = None
                    for qb in range(4):
                        for wh in range(2):
                            ins = e.matmul(bank(ob)[:, qb * 65:qb * 65 + 65], lhsT=PT[:, pi, (2 * qb + wh) * 128:(2 * qb + wh + 1) * 128],
                                           rhs=vaug[:, qb + wh, kv, :], start=(wh == 0), stop=(wh == 1))
                    return ins
                A("pe", fnpv, r=[B_PT[pi]] + B_v, w=[BK[ob]])
                Ov = bank(ob)[:, 0:260].rearrange("p (q d) -> p q d", d=65)
                di = h % 2
                A("act", lambda e, Ov=Ov, di=di, h=h: e.activation(out=dn[:, di, 0:4], in_=Ov[:, :, 64], func=AF.Identity, bias=es_sb[:, h:h + 1]),
                  r=[BK[ob], B_const], w=[B_dn[di]])
                A("act", lambda e, Ov=Ov, di=di: e.activation(out=Osb[:, di, :].rearrange("p (q d) -> p q d", d=64), in_=Ov[:, :, 0:64], func=AF.Copy),
                  r=[BK[ob]], w=[B_Osb[di]])
                A("dve", lambda e, di=di: e.reciprocal(out=dn[:, di, 4:8], in_=dn[:, di, 0:4]), r=[], w=[B_dn[di]])
                A("dve", lambda e, di=di, h=h: e.tensor_tensor(out=attn_tm[:, :, h * 64:(h + 1) * 64], in0=Osb[:, di, :].rearrange("p (q d) -> p q d", d=64),
                                                               in1=dn[:, di, 4:8].unsqueeze(2).broadcast_to([128, 4, 64]), op=ALU.mult),
                  r=[B_Osb[di], B_dn[di]], w=[B_ew[0], B_ew[1]])
            pend = None
            for h in range(16):
                emit_scores(h)
                if pend is not None:
                    emit_pv(pend)
                pend = h
            emit_pv(pend)
            for qb in range(4):
                transpose_to(lambda c, qb=qb: attn_tm[:, qb, c * 128:(c + 1) * 128], [B_ew[0], B_ew[1]], 128, 8, xnT[:, :, qb * 128:(qb + 1) * 128], [B_xnT[qb]])
            if not last:
                A("pool", lambda e: e.tensor_copy(out=kT[:, :, 0:128], in_=kT[:, :, 512:640]), r=[B_kT[4]], w=[B_kT[0]])
                A("pool", lambda e: e.tensor_copy(out=vaug[:, 0, :, 0:64], in_=vaug[:, 4, :, 0:64]), r=[B_v[4]], w=[B_v[0]])

            if STOP == 'attn' and last:
                return
            for g in range(4):
                b = banks.pair()
                A("pe", lambda e, g=g, b=b: e.matmul(pair(b)[:, 0:512], lhsT=Gw[:, g, :], rhs=m_sb[:, g, 0:512], start=True, stop=True)
                  if not has_s else
                  (e.matmul(pair(b)[:, 0:512], lhsT=Gw[:, g, :], rhs=m_sb[:, g, 0:512], start=True, stop=True),
                   e.matmul(pair(b)[:, 512:528], lhsT=Gw[:, g, :], rhs=m_sb[:, g, 512:528], start=True, stop=True))[1],
                  r=[B_m[g], B_const], w=[BK[b], BK[b + 1]])
                A("dve", lambda e, g=g, b=b: e.tensor_scalar(out=mixT[:, g, 0:ncol], in0=pair(b)[:, 0:ncol], scalar1=pscale[:, g:g + 1], scalar2=None,
                                                             op0=ALU.mult), r=[BK[b], BK[b + 1], B_const], w=[B_mix[g]])

            if STOP == 'attn2':
                return
            s_pb = next_unit()
            s_ab0 = next_unit()
            s_ab1 = next_unit()
            wpbv = rview(s_pb, 4, 1024)
            wabv = [rview(s_ab0, 8, 512), rview(s_ab1, 8, 512)]
            for mc in range(8):
                ba = banks.pair()

                def fna(e, ba=ba, mc=mc):
                    ins = None
                    for kc in range(4):
                        ins = e.matmul(pair(ba)[:, 0:512], lhsT=wpbv[:, kc, mc * 128:(mc + 1) * 128], rhs=mixT[:, kc, 0:512], start=(kc == 0), stop=(kc == 3))
                    return ins
                A("pe", fna, r=B_mix + [B_ring[s_pb]], w=[BK[ba], BK[ba + 1]])
                bb = banks.pair()
                wv_ = wabv[mc // 4]

                def fnb(e, bb=bb, mc=mc, wv_=wv_):
                    ins = None
                    mcl = mc % 4
                    for kc in range(8):
                        ins = e.matmul(pair(bb)[:, 0:512], lhsT=wv_[:, kc, mcl * 128:(mcl + 1) * 128], rhs=xnT[:, kc, 0:512], start=(kc == 0), stop=(kc == 7))
                    return ins
                A("pe", fnb, r=B_xnT[0:4] + [B_ring[s_ab0], B_ring[s_ab1]], w=[BK[bb], BK[bb + 1]])
                A("dve", lambda e, ba=ba, mc=mc: e.tensor_tensor(out=tmpf[:, 0, 0:512], in0=pair(ba)[:, 0:512], in1=sg[:, mc, 0:512], op=ALU.mult),
                  r=[BK[ba], BK[ba + 1], B_sg[mc]], w=[B_tmpf[0]])
                A("dve", lambda e, bb=bb, mc=mc: e.tensor_tensor(out=tmpf[:, 1, 0:512], in0=pair(bb)[:, 0:512], in1=sg[:, 8 + mc, 0:512], op=ALU.mult),
                  r=[BK[bb], BK[bb + 1], B_sg[8 + mc]], w=[B_tmpf[1]])
                A("dve", lambda e, mc=mc: e.tensor_tensor(out=mrgT[:, mc, 0:512], in0=tmpf[:, 0, 0:512], in1=tmpf[:, 1, 0:512], op=ALU.add),
                  r=[B_tmpf[0], B_tmpf[1]], w=[B_mrg[mc]])
            if has_s:
                for h in range(16):
                    kv, po = h // 4, ((h // 4) % 2) * 64
                    if (h % 2) * 64 == po:
                        src = qT[po:po + 64, h // 2, 512:528]
                        rb = [B_qT[4]]
                    else:
                        cpr = (h - 1) // 2 if po == 0 else (h - 2) // 2
                        src = qsh[po:po + 64, cpr, :]
                        rb = [SB["qsh"]]
                    A("act", lambda e, src=src, po=po, h=h: e.activation(out=qsel[po:po + 64, :, h], in_=src, func=AF.Copy), r=rb + [B_smp], w=[SB["qsel"]])
                bPS = banks.one()
                bOT = banks.one()
                banks.reserved = (bPS, bOT)
                OTv = bank(bOT)[:, 0:128].rearrange("p (c s) -> p c s", s=16)
                PTv = PTs.rearrange("p (sk pr two) -> p sk pr two", pr=2, two=2)
                ld_row(0)
                ld_row(1)
                for g4 in range(4):
                    cks_c, cvs_c, bk_c, bv_c = ckb[g4 % 2]
                    transpose_to(lambda c, cks_c=cks_c: cks_c[:, c % 4, (c // 4) * 128:(c // 4 + 1) * 128], bk_c, 128, 8, ksT_g, [SB["ksT"]])

                    def fsc(e, g4=g4):
                        ins = None
                        for sl in range(4):
                            s_ = g4 * 4 + sl
                            for kv in range(4):
                                po = (kv % 2) * 64
                                ins = e.matmul(bank(bPS)[:, s_ * 16 + kv * 4:s_ * 16 + kv * 4 + 4], lhsT=ksT_g[po:po + 64, (kv // 2) * 4 + sl, :],
                                               rhs=qsel[po:po + 64, s_, kv * 4:kv * 4 + 4], start=True, stop=True)
                        return ins
                    A("pe", fsc, r=[SB["ksT"], SB["qsel"]], w=[BK[bPS]])
                    A("act", lambda e, g4=g4: e.activation(out=PTs[:, g4 * 64:(g4 + 1) * 64], in_=bank(bPS)[:, g4 * 64:(g4 + 1) * 64], func=AF.Exp, scale=0.125),
                      r=[BK[bPS]], w=[SB["PTs"]])

                    def fpv(e, g4=g4, cvs_c=cvs_c):
                        ins = None
                        for sl in range(4):
                            s_ = g4 * 4 + sl
                            for kv in range(4):
                                for par in range(2):
                                    ins = e.matmul(OTv[par * 64:(par + 1) * 64, 2 * kv:2 * kv + 2, s_], lhsT=cvs_c[:, sl, kv * 64:(kv + 1) * 64],
                                                   rhs=PTv[:, s_ * 4 + kv, :, par], start=True, stop=True)
                        return ins
                    A("pe", fpv, r=[SB["PTs"], bv_c], w=[BK[bOT]])
                    if g4 + 2 < 4:
                        ld_cast(g4 + 2)
                        ld_row(g4 + 2)
                bD = banks.one()
                A("pe", lambda e: e.matmul(bank(bD)[:, 0:256], lhsT=ones_bf[:, :], rhs=PTs, start=True, stop=True), r=[SB["PTs"], B_const], w=[BK[bD]])
                A("act", lambda e: e.activation(out=rds, in_=bank(bD)[:, 0:256], func=AF.Copy), r=[BK[bD]], w=[SB["rds"]])
                A("dve", lambda e: e.tensor_tensor(out=rds.rearrange("p (s h) -> p s h", h=16), in0=rds.rearrange("p (s h) -> p s h", h=16),
                                                   in1=es_sb[:, 0:16].unsqueeze(1).broadcast_to([128, 16, 16]), op=ALU.add), r=[B_const], w=[SB["rds"]])
                A("dve", lambda e: e.reciprocal(out=rds, in_=rds), w=[SB["rds"]])
                A("act", lambda e: e.activation(out=OTsb, in_=bank(bOT)[:, 0:128], func=AF.Copy), r=[BK[bOT]], w=[SB["OTs"]])
                rdv = rds.rearrange("p (s c two) -> p c s two", c=8, two=2)
                OSv = OTsb.rearrange("p (c s) -> p c s", s=16)

                def fnn(e):
                    e.tensor_tensor(out=xnT[0:64, :, 512:528], in0=OSv[0:64], in1=rdv[0:64, :, :, 0], op=ALU.mult)
                    return e.tensor_tensor(out=xnT[64:128, :, 512:528], in0=OSv[64:128], in1=rdv[64:128, :, :, 1], op=ALU.mult)
                A("dve", fnn, r=[SB["OTs"], SB["rds"]], w=[B_xnT[4]])
                banks.reserved = ()
                A("pool", lambda e: e.memset(dummy[:, 1:2], 0.0), w=[B_smp] + list(SB.values()) + B_x[0:4] + [B_gate] + B_pT)

            if has_s:
                bsa = banks.one()
                bsb = banks.one()

                def fms(e, bsa=bsa, bsb=bsb):
                    ins = None
                    for mc in range(8):
                        for kc in range(4):
                            ins = e.matmul(bank(bsa)[:, mc * 16:(mc + 1) * 16], lhsT=wpbv[:, kc, mc * 128:(mc + 1) * 128], rhs=mixT[:, kc, 512:528],
                                           start=(kc == 0), stop=(kc == 3))
                    for mc in range(8):
                        wv_ = wabv[mc // 4]
                        mcl = mc % 4
                        for kc in range(8):
                            ins = e.matmul(bank(bsb)[:, mc * 16:(mc + 1) * 16], lhsT=wv_[:, kc, mcl * 128:(mcl + 1) * 128], rhs=xnT[:, kc, 512:528],
                                           start=(kc == 0), stop=(kc == 7))
                    return ins
                A("pe", fms, r=B_mix + [B_xnT[4], B_ring[s_pb], B_ring[s_ab0], B_ring[s_ab1]], w=[BK[bsa], BK[bsb]])
                tA = tmpf[:, 0, 0:128].rearrange("p (m c) -> p m c", c=16)
                tB = tmpf[:, 1, 0:128].rearrange("p (m c) -> p m c", c=16)
                A("dve", lambda e, bsa=bsa: e.tensor_tensor(out=tA, in0=bank(bsa)[:, 0:128].rearrange("p (m c) -> p m c", c=16), in1=sg[:, 0:8, 512:528], op=ALU.mult),
                  r=[BK[bsa]] + B_sg[0:8], w=[B_tmpf[0]])
                A("dve", lambda e, bsb=bsb: e.tensor_tensor(out=tB, in0=bank(bsb)[:, 0:128].rearrange("p (m c) -> p m c", c=16), in1=sg[:, 8:16, 512:528], op=ALU.mult),
                  r=[BK[bsb]] + B_sg[8:16], w=[B_tmpf[1]])
                A("dve", lambda e: e.tensor_tensor(out=mrgT[:, :, 512:528], in0=tA, in1=tB, op=ALU.add), r=[B_tmpf[0], B_tmpf[1]], w=B_mrg)

            if STOP == 'merge':
                return
            release(3)
            tblocks = [bl for bl in blocks if bl[0] != "h"]

            s_o = [next_unit(), next_unit()]
            wvo = [rview(s_o[0], 8, 512), rview(s_o[1], 8, 512)]
            for (kind, nt, xap, xbuf, col0, cb, trow) in tblocks:
                for hf in range(2):
                    b = banks.one()

                    def fn(e, b=b, col0=col0, nt=nt, hf=hf):
                        ins = None
                        for kc in range(8):
                            ins = e.matmul(bank(b)[:nt, :], lhsT=mrgT[:, kc, col0:col0 + nt], rhs=wvo[hf][:, kc, :], start=(kc == 0), stop=(kc == 7))
                        return ins
                    A("pe", fn, r=B_mrg + [B_ring[s_o[hf]]], w=[BK[b]])
                    A("dve", lambda e, b=b, xap=xap, hf=hf, nt=nt: e.tensor_tensor(out=xap[:, hf * 512:(hf + 1) * 512], in0=bank(b)[:nt, :],
                                                                                  in1=xap[:, hf * 512:(hf + 1) * 512], op=ALU.add),
                      r=[BK[b]], w=[xbuf])
            release(2)
            if STOP == 'wout':
                return
            for grp in (tblocks[0:4], tblocks[4:]):
                tis = []
                for (kind, nt, xap, xbuf, col0, cb, trow) in grp:
                    ti = ntm()
                    tis.append(ti)
                    rmsnorm_to_bf16(xap, xbuf, nt, 1, tmst[:nt, ti, :], B_tm[ti])
                for ti, (kind, nt, xap, xbuf, col0, cb, trow) in zip(tis, grp):
                    transpose_to(lambda c, ti=ti, nt=nt: tmst[:nt, ti, c * 128:(c + 1) * 128], B_tm[ti], nt, 8,
                                 xnT[:, :, col0:col0 + nt], [B_xnT[cb]])
            if STOP == 'ln2':
                return
            if t + 1 < NT:
                nslot = (t + 1) % 2
                tis_n = []
                for b_ in range(4):
                    ti = ntm()
                    tis_n.append(ti)
                    rmsnorm_to_bf16(x_tm[:, nslot * 4 + b_, :], B_x[nslot * 4 + b_], 128, 0, tmst[:, ti, :], B_tm[ti])
                pre_tis[t + 1] = tis_n
            def actT(hc):
                return sg[:, hc, :] if hc < 16 else qT[:, hc - 16, :]

            def actB(hc):
                return [B_sg[hc]] if hc < 16 else list(B_qT)
            for n in range(11):
                s_f = next_unit()
                wv_ = rview(s_f, 8, 512)
                for l in range(2):
                    hc = 2 * n + l
                    bg = banks.pair()
                    bu = banks.pair()

                    def fn(e, bg=bg, bu=bu, l=l, wv_=wv_):
                        ins = None
                        for (bb_, coff) in ((bg, l * 128), (bu, 256 + l * 128)):
                            for kc in range(8):
                                ins = e.matmul(pair(bb_)[:, 0:512], lhsT=wv_[:, kc, coff:coff + 128], rhs=xnT[:, kc, 0:512], start=(kc == 0), stop=(kc == 7))
                            if has_s:
                                for kc in range(8):
                                    ins = e.matmul(pair(bb_)[:, 512:528], lhsT=wv_[:, kc, coff:coff + 128], rhs=xnT[:, kc, 512:528],
                                                   start=(kc == 0), stop=(kc == 7))
                        return ins
                    A("pe", fn, r=hn_all + [B_ring[s_f]], w=[BK[bg], BK[bg + 1], BK[bu], BK[bu + 1]])
                    tf = hc % 2
                    A("act", lambda e, bg=bg, tf=tf: e.activation(out=tmpf[:, tf, 0:ncol], in_=pair(bg)[:, 0:ncol], func=AF.Silu),
                      r=[BK[bg], BK[bg + 1]], w=[B_tmpf[tf]])
                    A("dve", lambda e, bu=bu, tf=tf, hc=hc: e.tensor_tensor(out=actT(hc)[:, 0:ncol], in0=pair(bu)[:, 0:ncol], in1=tmpf[:, tf, 0:ncol], op=ALU.mult),
                      r=[BK[bu], BK[bu + 1], B_tmpf[tf]], w=actB(hc))
                release(1)
            if STOP == 'ffi':
                return
            allact = list(B_sg) + list(B_qT)
            for hf in range(2):
                bks = [banks.one() for _ in tblocks]
                for ui, k0 in enumerate((0, 8, 16)):
                    s_f = next_unit()
                    nkc = min(8, NHC - k0)
                    wv_ = rview(s_f, nkc, 512)
                    for bi, (kind, nt, xap, xbuf, col0, cb, trow) in enumerate(tblocks):
                        b = bks[bi]

                        def fn(e, b=b, col0=col0, nt=nt, wv_=wv_, k0=k0, nkc=nkc):
                            ins = None
                            for kk in range(nkc):
                                ins = e.matmul(bank(b)[:nt, :], lhsT=actT(k0 + kk)[:, col0:col0 + nt], rhs=wv_[:, kk, :],
                                               start=(k0 + kk == 0), stop=(k0 + kk == NHC - 1))
                            return ins
                        A("pe", fn, r=allact + [B_ring[s_f]], w=[BK[b]])
                    release(1)
                for bi, (kind, nt, xap, xbuf, col0, cb, trow) in enumerate(tblocks):
                    b = bks[bi]
                    A("dve", lambda e, b=b, xap=xap, hf=hf, nt=nt: e.tensor_tensor(out=xap[:, hf * 512:(hf + 1) * 512], in0=bank(b)[:nt, :],
                                                                                  in1=xap[:, hf * 512:(hf + 1) * 512], op=ALU.add),
                      r=[BK[b]], w=[xbuf])
            if STOP == 'ffo':
                return
            if t + 1 < NT and (t + 1) in pre_tis:
                for b_, ti in enumerate(pre_tis[t + 1]):
                    transpose_to(lambda c, ti=ti: tmst[:, ti, c * 128:(c + 1) * 128], B_tm[ti], 128, 8, qT[:, :, b_ * 128:(b_ + 1) * 128], [B_qT[b_]])
                done_TX[t + 1] = True
                if DO_SAMPLE and t + 1 == NT - 1:
                    ti = ntm()
                    rmsnorm_to_bf16(x_aux[:NSMP, :], B_xaux, NSMP, 0, tmst[:NSMP, ti, :], B_tm[ti])
                    transpose_to(lambda c, ti=ti: tmst[:NSMP, ti, c * 128:(c + 1) * 128], B_tm[ti], NSMP, 8, qT[:, :, 512:512 + NSMP], [B_qT[4]])
                    done_aux[t + 1] = True
            for bi, (kind, nt, xap, xbuf, col0, cb, trow) in enumerate(tblocks):
                transpose_f32_to(lambda c, xap=xap: xap[:, c * 128:(c + 1) * 128], [xbuf], nt, 8, xnT[:, :, col0:col0 + nt], [B_xnT[cb]])
                pidx = bi if kind == "p" else 4
                transpose_f32_to(lambda c, pidx=pidx, nt=nt: p_tm[:nt, pidx, c * 128:(c + 1) * 128], [B_p[pidx]], nt, 2,
                                 pT[:, :, col0:col0 + nt], [B_pT[cb]])
            s_g0 = next_unit()
            s_g1 = next_unit()
            s_pp = next_unit()
            wg = [rview(s_g0, 8, 512), rview(s_g1, 8, 512)]
            wppv = rview(s_pp, 2, 1024)
            for bi, (kind, nt, xap, xbuf, col0, cb, trow) in enumerate(tblocks):
                bgp = banks.pair()

                def fng(e, bgp=bgp, col0=col0, nt=nt):
                    ins = None
                    for hf in range(2):
                        for kc in range(8):
                            ins = e.matmul(pair(bgp)[:nt, hf * 512:(hf + 1) * 512], lhsT=xnT[:, kc, col0:col0 + nt], rhs=wg[hf][:, kc, :],
                                           start=(kc == 0), stop=(kc == 7))
                    return ins
                A("pe", fng, r=[B_xnT[cb], B_ring[s_g0], B_ring[s_g1]], w=[BK[bgp], BK[bgp + 1]])
                if bi % 2 == 0:
                    gt_ap, gt_b = gate_tm[:nt, :], [B_gate]
                else:
                    gt_ap, gt_b = Osb[:nt].rearrange("p a d -> p (a d)").bitcast(BF16), [B_Osb[0], B_Osb[1]]
                A("act", lambda e, bgp=bgp, nt=nt, gt_ap=gt_ap: e.activation(out=gt_ap, in_=pair(bgp)[:nt, :], func=AF.Sigmoid),
                  r=[BK[bgp], BK[bgp + 1]], w=gt_b)
                bep = banks.pair()

                def fne(e, bep=bep, col0=col0, nt=nt):
                    ins = None
                    for hf in range(2):
                        for kc in range(2):
                            ins = e.matmul(pair(bep)[:nt, hf * 512:(hf + 1) * 512], lhsT=pT[:, kc, col0:col0 + nt], rhs=wppv[:, kc, hf * 512:(hf + 1) * 512],
                                           start=(kc == 0), stop=(kc == 1))
                    return ins
                A("pe", fne, r=[B_pT[cb], B_ring[s_pp]], w=[BK[bep], BK[bep + 1]])
                ei = bi % 2
                ew = ework[:nt, ei, :]
                def ecopy(e, bep=bep, ew=ew, nt=nt):
                    e.tensor_copy(out=ew[:, 0:512], in_=pair(bep)[:nt, 0:512])
                    return e.tensor_copy(out=ew[:, 512:1024], in_=pair(bep)[:nt, 512:1024])
                A("dve", ecopy, r=[BK[bep], BK[bep + 1]], w=[B_ew[ei]])
                si = nscal()
                c = si * 3
                A("act", lambda e, ew=ew, c=c, nt=nt, bep=bep: e.activation(out=pair(bep)[:nt, :], in_=ew, func=AF.Square, accum_out=scal[:nt, c:c + 1]),
                  r=[B_ew[ei]], w=[BK[bep], BK[bep + 1], B_scal[si]])
                A("dve", lambda e, c=c, nt=nt: e.tensor_scalar(out=scal[:nt, c + 1:c + 2], in0=scal[:nt, c:c + 1], scalar1=1.0 / D, scalar2=EPS, op0=ALU.mult, op1=ALU.add),
                  w=[B_scal[si]])
                A("pool", lambda e, c=c, nt=nt: e.tensor_tensor(out=scal[:nt, c + 2:c + 3], in0=scal[:nt, c + 1:c + 2], in1=nhalf[:nt, :], op=ALU.pow), r=[B_ca], w=[B_scal[si]])
                A("dve", lambda e, ew=ew, c=c, nt=nt: e.scalar_tensor_tensor(out=ew, in0=ew, scalar=scal[:nt, c + 2:c + 3], in1=gains[:nt, 2, :],
                                                                            op0=ALU.mult, op1=ALU.mult), r=[B_scal[si], B_const], w=[B_ew[ei]])
                A("dve", lambda e, ew=ew, nt=nt, gt_ap=gt_ap: e.tensor_tensor(out=ew, in0=ew, in1=gt_ap, op=ALU.mult), r=gt_b, w=[B_ew[ei]])
                A("dve", lambda e, ew=ew, xap=xap: e.tensor_tensor(out=xap, in0=xap, in1=ew, op=ALU.add), r=[B_ew[ei]], w=[xbuf])
                def tail_fn(kind=kind, nt=nt, xap=xap, xbuf=xbuf, bi=bi, ew=ew, ei=ei, t=t):
                    si = nscal()
                    c = si * 3
                    yo = xap
                    A("act", lambda e, xap=xap, c=c, nt=nt, ew=ew: e.activation(out=ew, in_=xap, func=AF.Square, accum_out=scal[:nt, c:c + 1]),
                      r=[xbuf], w=[B_ew[ei], B_scal[si]])
                    A("dve", lambda e, c=c, nt=nt: e.tensor_scalar(out=scal[:nt, c + 1:c + 2], in0=scal[:nt, c:c + 1], scalar1=1.0 / D, scalar2=EPS, op0=ALU.mult, op1=ALU.add),
                      w=[B_scal[si]])
                    A("pool", lambda e, c=c, nt=nt: e.tensor_tensor(out=scal[:nt, c + 2:c + 3], in0=scal[:nt, c + 1:c + 2], in1=nhalf[:nt, :], op=ALU.pow), r=[B_ca], w=[B_scal[si]])
                    A("dve", lambda e, yo=yo, xap=xap, c=c, nt=nt: e.scalar_tensor_tensor(out=yo, in0=xap, scalar=scal[:nt, c + 2:c + 3], in1=gains[:nt, 3, :],
                                                                                         op0=ALU.mult, op1=ALU.mult), r=[B_scal[si], B_const], w=[xbuf])
                    if kind == "p":
                        r0 = t * 512 + bi * 128
                        A("sp", lambda e, yo=yo, r0=r0: [e.dma_start(out=y[r0:r0 + 128, :], in_=yo)], r=[xbuf, B_out], dsem=ds_yo[(t % 2) * 4 + bi])
                    else:
                        A("sp", lambda e, yo=yo: [e.dma_start(out=ys, in_=yo)], r=[xbuf, B_out], dsem=ds_yo[8])
                if last:
                    tail_fn()
                else:
                    tail_ops.append(tail_fn)
            release(3)

        for t_ in range(NT):
            do_tile(t_)
        while tail_ops:
            tail_ops.pop(0)()

        if DO_SAMPLE:
            A("sp", lambda e: [e.dma_start(out=nks[:, 0:127, :], in_=ck[:, 1:128, :]), e.dma_start(out=nvs[:, 0:127, :], in_=cv[:, 1:128, :])],
              r=[B_out], dsem=ds_smp, ndma=2)
        A("sp", lambda e: None, r=B_ring + B_x + B_p + [B_xaux, B_const] + B_scr, w=[B_out])

        S.finalize(sems)
        block = es.enter_context(nc.Block())

        @block.tensor
        def _(e):
            S.emit("pe", e)

        @block.scalar
        def _(e):
            S.emit("act", e)

        @block.vector
        def _(e):
            S.emit("dve", e)

        @block.gpsimd
        def _(e):
            S.emit("pool", e)

        @block.sync
        def _(e):
            S.emit("sp", e)
    return nc


def sample_attention(L):
    raise NotImplementedError


_PROG = None


def _tables():
    half = 8
    inv = (500000.0 ** (-(np.arange(0, 16, 2, dtype=np.float32) / np.float32(16)))).astype(np.float32)
    return inv


def kernel(**inp):
    global _PROG
    bf = ml_dtypes.bfloat16
    x_prompt = np.asarray(inp["x_prompt"], np.float32)
    x_sample = np.asarray(inp["x_sample"], np.float32)
    p_prompt = np.asarray(inp["p_prompt"], np.float32)
    p_sample = np.asarray(inp["p_sample"], np.float32)
    cache_k = np.asarray(inp["cache_k"], np.float32)
    cache_v = np.asarray(inp["cache_v"], np.float32)
    state_pool = np.asarray(inp["state_pool"], np.float32)
    if _PROG is None:
        _PROG = build_program()
    nc = _PROG
    inv = _tables()
    kk = np.arange(128)[:, None]
    qq = np.arange(128)[None, :]
    m_prev = np.where(kk > qq, 0.0, -30000.0).astype(np.float32)
    m_diag = np.where(kk <= qq, 0.0, -30000.0).astype(np.float32)
    mask1 = np.concatenate([m_prev, m_diag] * 4, axis=1)
    ident = np.eye(128, dtype=np.float32).astype(bf)
    shared = {}
    for k_ in ("ln1", "w_in", "pool_group_w", "pool_scale", "attn_sinks", "w_pool_branch", "w_attn_branch", "w_out", "ln2",
               "w_ffn_in", "w_ffn_out", "w_ple_proj", "ple_norm", "w_ple_gate"):
        shared[k_] = np.ascontiguousarray(np.asarray(inp[k_], np.float32)[0])
    shared["final_norm"] = np.ascontiguousarray(np.asarray(inp["final_norm"], np.float32))
    shared["ident"] = ident
    shared["identf"] = np.eye(128, dtype=np.float32)
    selm = np.zeros((128, 2, 4, 16), np.float32)
    for hf in range(2):
        for s8 in range(8):
            for r_ in range(15):
                for g in range(4):
                    if r_ >= 16 - (2 << g):
                        selm[s8 * 15 + r_, hf, g, hf * 8 + s8] = 1.0
    shared["sel"] = selm.reshape(128, 128)
    in_maps = []
    for c in range(NCORE):
        bi, j = c // 4, c % 4
        t0 = j * TOKC
        xh = np.zeros((TOKC + 128, D), np.float32)
        xh[128:] = x_prompt[bi, t0:t0 + TOKC]
        if j > 0:
            xh[:128] = x_prompt[bi, t0 - 128:t0]
        pos = np.concatenate([np.arange(t0 - 128, t0 + TOKC), np.full(128, 16384)]).astype(np.float32)
        ang = pos[:, None] * inv[None, :]
        cost = np.cos(ang).astype(np.float32)
        sint = np.sin(ang).astype(np.float32)
        mask0 = mask1.copy()
        if j == 0:
            mask0[:, 0:128] = -30000.0
        maskd = np.stack([mask0, mask1], axis=1).astype(bf)
        rc = np.zeros((4, 16), np.float32)
        for g in range(4):
            w_ = 2 << g
            for tt in range(16):
                rc[g, tt] = 1.0 / (min(w_, tt + 1) if j == 0 else w_)
        rc16 = np.ascontiguousarray(np.broadcast_to(rc.reshape(1, 64), (128, 64)))
        m = dict(shared)
        m.update({
            "xh": xh, "ph": np.ascontiguousarray(p_prompt[0, bi, t0:t0 + TOKC]),
            "xs": np.ascontiguousarray(x_sample[c * NSMP:(c + 1) * NSMP, 0]),
            "ps": np.ascontiguousarray(p_sample[0, c * NSMP:(c + 1) * NSMP, 0]),
            "ck": np.ascontiguousarray(cache_k[0, c * NSMP:(c + 1) * NSMP].reshape(NSMP, 128, 256)),
            "cv": np.ascontiguousarray(cache_v[0, c * NSMP:(c + 1) * NSMP].reshape(NSMP, 128, 256)),
            "spool": np.ascontiguousarray(state_pool[0, c * NSMP:(c + 1) * NSMP]),
            "cs": np.ascontiguousarray(np.stack([cost.reshape(18, 128, 8), sint.reshape(18, 128, 8)], axis=0).transpose(2, 0, 1, 3).reshape(128, 288)),
            "maskd": maskd, "rc16": rc16,
        })
        in_maps.append(m)
    res = run_bass_kernel_spmd(nc, in_maps, core_ids=list(range(NCORE)))
    R = res.results
    y_prompt = np.stack([np.concatenate([R[b * 4 + j]["y"] for j in range(4)], axis=0) for b in range(2)], axis=0)
    y_sample = np.concatenate([R[c]["ys"] for c in range(NCORE)], axis=0).reshape(128, 1, D)
    nkp = np.stack([R[3]["nk"], R[7]["nk"]], axis=0).reshape(1, 2, 128, 4, 64)
    nvp = np.stack([R[3]["nv"], R[7]["nv"]], axis=0).reshape(1, 2, 128, 4, 64)
    npp = np.stack([R[3]["npool"][1:16], R[7]["npool"][1:16]], axis=0).reshape(1, 2, 15, 512)
    nks = np.concatenate([R[c]["nks"] for c in range(NCORE)], axis=0).reshape(1, 128, 128, 4, 64)
    nvs = np.concatenate([R[c]["nvs"] for c in range(NCORE)], axis=0).reshape(1, 128, 128, 4, 64)
    nps = np.concatenate([R[c]["nps"] for c in range(NCORE)], axis=0).reshape(1, 128, 15, 512)
    f = lambda a: np.ascontiguousarray(a, dtype=np.float32)
    return (f(y_prompt), f(y_sample), f(nkp), f(nvp), f(npp), f(nks), f(nvs), f(nps))
```
